# Optimizing a Trainium2 kernel written in Bass

```python
import math
import jax, jax.numpy as jnp
from jax import lax
import numpy as np


D_MODEL = 1024
BATCH = 2
SEQ = 8192
DEPTH = 2

CHUNK = 64
Q_BLOCK = 128
EPS = 1e-6
NEG = -1e30

FOX_HEADS = 4
FOX_DIM = 64
DIFF_HEADS = 4
DIFF_QK = 64
DIFF_V = 2 * DIFF_QK
MLA_HEADS = 4
MLA_NOPE = 64
MLA_ROPE = 32
MLA_V = 64
MLA_Q_RANK = 256
MLA_KV_RANK = 128
ROPE_THETA = 10000.0
D_FF = 2816
CONV_W = 3

MIX_WIDTH = FOX_HEADS * FOX_DIM + DIFF_HEADS * DIFF_V + MLA_HEADS * MLA_V
IN_SIZES = (FOX_HEADS * FOX_DIM, FOX_HEADS * FOX_DIM, FOX_HEADS * FOX_DIM, FOX_HEADS,
            DIFF_HEADS * 2 * DIFF_QK, DIFF_HEADS * 2 * DIFF_QK, DIFF_HEADS * DIFF_V,
            MLA_Q_RANK, MLA_KV_RANK, MLA_ROPE)
IN_COLS = sum(IN_SIZES)

kernel_name = 'hybrid_fox_diff_mla_convffn'


def rmsnorm(x, g):
    xf = x.astype(jnp.float32)
    y = xf * lax.rsqrt(jnp.mean(xf * xf, axis=-1, keepdims=True) + EPS)
    return (y * g.astype(jnp.float32)).astype(x.dtype)


def to_heads(t, h):
    b, s, _ = t.shape
    return t.reshape(b, s, h, -1).transpose(0, 2, 1, 3)


def merge_heads(t):
    b, h, s, d = t.shape
    return t.transpose(0, 2, 1, 3).reshape(b, s, h * d)


def rope(x, pos):
    half = x.shape[-1] // 2
    inv = 1.0 / (ROPE_THETA ** (jnp.arange(half, dtype=jnp.float32) / half))
    ang = pos[:, None] * inv[None, :]
    cos, sin = jnp.cos(ang), jnp.sin(ang)
    xf = x.astype(jnp.float32)
    x1, x2 = xf[..., :half], xf[..., half:]
    return jnp.concatenate([x1 * cos - x2 * sin, x1 * sin + x2 * cos], axis=-1).astype(x.dtype)


def block_attention(q, k, v, bias_fn, scale):
    b, h, s, _ = q.shape
    dv = v.shape[-1]
    ki = jnp.arange(s)

    def one(blk):
        start = blk * Q_BLOCK
        qb = lax.dynamic_slice_in_dim(q, start, Q_BLOCK, axis=2)
        qi = start + jnp.arange(Q_BLOCK)
        sc = jnp.einsum('bhqd,bhkd->bhqk', qb, k).astype(jnp.float32) * scale
        p = jax.nn.softmax(sc + bias_fn(qi, ki), axis=-1).astype(v.dtype)
        return jnp.einsum('bhqk,bhkd->bhqd', p, v)

    out = lax.map(one, jnp.arange(s // Q_BLOCK))
    return out.transpose(1, 2, 0, 3, 4).reshape(b, h, s, dv)


def chunk_mask(qi, ki):
    return (ki[None, :] // CHUNK) <= (qi[:, None] // CHUNK)


def fox_group(q, k, v, f_logit, f_bias):
    q, k, v = to_heads(q, FOX_HEADS), to_heads(k, FOX_HEADS), to_heads(v, FOX_HEADS)
    logf = jax.nn.log_sigmoid((f_logit + f_bias).astype(jnp.float32))
    cum = jnp.cumsum(logf, axis=1).transpose(0, 2, 1)

    def bias(qi, ki):
        cq = jnp.take(cum, qi, axis=2)
        d = cq[..., :, None] - cum[..., None, :]
        return jnp.where((ki[None, :] <= qi[:, None])[None, None], d, NEG)

    return merge_heads(block_attention(q, k, v, bias, FOX_DIM ** -0.5))


def diff_group(q, k, v, lq1, lk1, lq2, lk2, norm_g, lam_init):
    q, k, v = to_heads(q, DIFF_HEADS), to_heads(k, DIFF_HEADS), to_heads(v, DIFF_HEADS)
    q1, q2 = q[..., :DIFF_QK], q[..., DIFF_QK:]
    k1, k2 = k[..., :DIFF_QK], k[..., DIFF_QK:]
    slopes = 2.0 ** (-8.0 * jnp.arange(1, DIFF_HEADS + 1, dtype=jnp.float32) / DIFF_HEADS)

    def bias(qi, ki):
        dist = jnp.abs(qi[:, None] - ki[None, :]).astype(jnp.float32)
        alibi = -slopes[:, None, None] * dist[None]
        return jnp.where(chunk_mask(qi, ki)[None], alibi, NEG)[None]

    scale = DIFF_QK ** -0.5
    o1 = block_attention(q1, k1, v, bias, scale)
    o2 = block_attention(q2, k2, v, bias, scale)
    lam = (jnp.exp(jnp.sum(lq1.astype(jnp.float32) * lk1.astype(jnp.float32)))
           - jnp.exp(jnp.sum(lq2.astype(jnp.float32) * lk2.astype(jnp.float32))) + lam_init)
    o = o1 - lam.astype(o1.dtype) * o2
    o = rmsnorm(o, norm_g) * (1.0 - lam_init)
    return merge_heads(o)


def mla_group(c_q, c_kv, k_rope_raw, q_norm_g, w_uq, kv_norm_g, w_ukv, pos):
    q = to_heads(rmsnorm(c_q, q_norm_g) @ w_uq, MLA_HEADS)
    q = jnp.concatenate([q[..., :MLA_NOPE], rope(q[..., MLA_NOPE:], pos)], axis=-1)
    kv = to_heads(rmsnorm(c_kv, kv_norm_g) @ w_ukv, MLA_HEADS)
    k_nope, v = kv[..., :MLA_NOPE], kv[..., MLA_NOPE:]
    k_r = rope(k_rope_raw[:, None], pos)
    k = jnp.concatenate([k_nope, jnp.broadcast_to(k_r, k_nope.shape[:3] + (MLA_ROPE,))], axis=-1)

    def bias(qi, ki):
        return jnp.where(chunk_mask(qi, ki), 0.0, NEG)[None, None]

    return merge_heads(block_attention(q, k, v, bias, (MLA_NOPE + MLA_ROPE) ** -0.5))


def causal_dwconv(u, w, b):
    s = u.shape[1]
    up = jnp.pad(u, ((0, 0), (CONV_W - 1, 0), (0, 0)))
    y = b
    for j in range(CONV_W):
        y = y + up[:, j:j + s, :] * w[j]
    return y


def setup_inputs(seed: int = 0) -> dict:
    key = jax.random.key(seed)
    ks = jax.random.split(key, 24)
    n = jax.random.normal
    f32 = jnp.float32
    return {
        'x': n(ks[0], (BATCH, SEQ, D_MODEL), f32),
        'ln1_g': 1.0 + 0.02 * n(ks[1], (DEPTH, D_MODEL), f32),
        'w_in': n(ks[2], (DEPTH, D_MODEL, IN_COLS), f32) * D_MODEL ** -0.5,
        'fgate_b': 2.0 + 0.5 * n(ks[3], (DEPTH, FOX_HEADS), f32),
        'lam_q1': 0.1 * n(ks[4], (DEPTH, DIFF_QK), f32),
        'lam_k1': 0.1 * n(ks[5], (DEPTH, DIFF_QK), f32),
        'lam_q2': 0.1 * n(ks[6], (DEPTH, DIFF_QK), f32),
        'lam_k2': 0.1 * n(ks[7], (DEPTH, DIFF_QK), f32),
        'diff_norm_g': 1.0 + 0.02 * n(ks[8], (DEPTH, DIFF_V), f32),
        'q_norm_g': 1.0 + 0.02 * n(ks[9], (DEPTH, MLA_Q_RANK), f32),
        'w_uq': n(ks[10], (DEPTH, MLA_Q_RANK, MLA_HEADS * (MLA_NOPE + MLA_ROPE)), f32) * MLA_Q_RANK ** -0.5,
        'kv_norm_g': 1.0 + 0.02 * n(ks[11], (DEPTH, MLA_KV_RANK), f32),
        'w_ukv': n(ks[12], (DEPTH, MLA_KV_RANK, MLA_HEADS * (MLA_NOPE + MLA_V)), f32) * MLA_KV_RANK ** -0.5,
        'w_o': n(ks[13], (DEPTH, MIX_WIDTH, D_MODEL), f32) * MIX_WIDTH ** -0.5,
        'ln2_g': 1.0 + 0.02 * n(ks[14], (DEPTH, D_MODEL), f32),
        'w_up': n(ks[15], (DEPTH, D_MODEL, 2 * D_FF), f32) * D_MODEL ** -0.5,
        'conv_w': n(ks[16], (DEPTH, CONV_W, 2 * D_FF), f32) * CONV_W ** -0.5,
        'conv_b': 0.02 * n(ks[17], (DEPTH, 2 * D_FF), f32),
        'w_down': n(ks[18], (DEPTH, D_FF, D_MODEL), f32) * D_FF ** -0.5,
        'final_g': 1.0 + 0.02 * n(ks[19], (D_MODEL,), f32),
    }


def reference(x, ln1_g, w_in, fgate_b, lam_q1, lam_k1, lam_q2, lam_k2, diff_norm_g,
              q_norm_g, w_uq, kv_norm_g, w_ukv, w_o, ln2_g, w_up, conv_w, conv_b,
              w_down, final_g):
    pos = jnp.arange(x.shape[1], dtype=jnp.float32)
    split_points = np.cumsum(IN_SIZES)[:-1].tolist()
    for i in range(DEPTH):
        lam_init = 0.8 - 0.6 * math.exp(-0.3 * i)
        h = rmsnorm(x, ln1_g[i]) @ w_in[i]
        (fq, fk, fv, ff, dq, dk, dv, c_q, c_kv, k_r) = jnp.split(h, split_points, axis=-1)
        o_fox = fox_group(fq, fk, fv, ff, fgate_b[i])
        o_diff = diff_group(dq, dk, dv, lam_q1[i], lam_k1[i], lam_q2[i], lam_k2[i],
                            diff_norm_g[i], lam_init)
        o_mla = mla_group(c_q, c_kv, k_r, q_norm_g[i], w_uq[i], kv_norm_g[i], w_ukv[i], pos)
        x = x + jnp.concatenate([o_fox, o_diff, o_mla], axis=-1) @ w_o[i]
        up = causal_dwconv(rmsnorm(x, ln2_g[i]) @ w_up[i], conv_w[i], conv_b[i])
        g, u = up[..., :D_FF], up[..., D_FF:]
        x = x + (jax.nn.silu(g) * u) @ w_down[i]
    return rmsnorm(x, final_g)
```

```python
import numpy as np
import ml_dtypes
import concourse.bass as bass
import concourse.mybir as mybir
from concourse.bass_utils import run_bass_kernel_spmd

F32 = mybir.dt.float32
BF16 = mybir.dt.bfloat16
ALU = mybir.AluOpType
AF = mybir.ActivationFunctionType
AX = mybir.AxisListType

D = 1024
SEQ = 8192
NB = 2
DEPTH = 2
DFF = 2816
EPS = 1e-6
NEG = -30000.0
TOK = 2048
NCORE = 8


class Buf:
    __slots__ = ("w", "r", "name")

    def __init__(self, name=""):
        self.w = None
        self.r = []
        self.name = name


class Sched:
    def __init__(self, nc, tag=""):
        self.nc = nc
        self.E = {"pe": nc.tensor, "act": nc.scalar, "dve": nc.vector, "pool": nc.gpsimd, "sp": nc.sync}
        self.sem = {}
        self.cnt = {}
        self.seen = {k: {} for k in self.E}
        for k in self.E:
            self.sem[k] = nc.alloc_semaphore(f"s_{tag}{k}")
            self.cnt[k] = 0
        self.tag = tag
        self.ndma = 0
        self.dma_sems = {}

    def _wait(self, e, tok):
        if tok is None:
            return
        sem, val, key = tok
        if self.seen[e].get(key, 0) >= val:
            return
        self.E[e].wait_ge(sem, val)
        self.seen[e][key] = val

    def _deps(self, e, r, w):
        for b in r:
            self._wait(e, b.w)
        for b in w:
            self._wait(e, b.w)
            for t in b.r:
                self._wait(e, t)

    def _mark(self, tok, r, w):
        for b in r:
            b.r.append(tok)
        for b in w:
            b.w = tok
            b.r = []

    def op(self, e, fn, r=(), w=()):
        self._deps(e, r, w)
        fns = fn if isinstance(fn, (list, tuple)) else [fn]
        ins = None
        for f in fns:
            ins = f(self.E[e])
        self.cnt[e] += 1
        assert self.cnt[e] < 60000
        ins.then_inc(self.sem[e], 1)
        tok = (self.sem[e], self.cnt[e], e)
        self._mark(tok, r, w)
        return tok

    def dma(self, q, out, in_, r=(), w=(), slot=None):
        self._deps(q, r, w)
        if slot is None:
            slot = f"d{self.ndma % 8}"
        self.ndma += 1
        if slot not in self.dma_sems:
            self.dma_sems[slot] = [self.nc.alloc_semaphore(f"d_{self.tag}{slot}"), 0]
        ent = self.dma_sems[slot]
        if ent[1] > 0:
            self._wait(q, (ent[0], ent[1], "dma" + slot))
        ent[1] += 16
        assert ent[1] < 60000
        self.E[q].dma_start(out=out, in_=in_).then_inc(ent[0], 16)
        tok = (ent[0], ent[1], "dma" + slot)
        self._mark(tok, r, w)
        return tok

    def barrier(self):
        toks = [(self.sem[k], self.cnt[k], k) for k in self.E if self.cnt[k] > 0]
        toks += [(ent[0], ent[1], "dma" + slot) for slot, ent in self.dma_sems.items() if ent[1] > 0]
        for e in self.E:
            for t in toks:
                if t[2] != e:
                    self._wait(e, t)

    def finish(self, toks):
        for t in toks:
            self._wait("sp", t)


class Ctx:
    def __init__(self, nc, tag=""):
        self.nc = nc
        self.S = Sched(nc, tag)
        self.stack = []

    def sb(self, name, shape, dt):
        self.n = getattr(self, "n", 0) + 1
        g = self.nc.sbuf_tensor(f"{name}_{self.n}", list(shape), dt)
        t = g.__enter__()
        self.stack.append(g)
        return t

    def ps(self, name, shape, dt):
        self.n = getattr(self, "n", 0) + 1
        g = self.nc.psum_tensor(f"{name}_{self.n}", list(shape), dt)
        t = g.__enter__()
        self.stack.append(g)
        return t

    def release(self, mark=0):
        while len(self.stack) > mark:
            self.stack.pop().__exit__(None, None, None)


def emit_rstd(S, ss, r, nparts, n, bss, br, tmp=None):
    S.op("act", lambda e: e.activation(out=ss, in_=ss, func=AF.Sqrt, bias=float(EPS), scale=1.0 / n), r=[bss], w=[bss])
    S.op("dve", lambda e: e.reciprocal(out=r, in_=ss), r=[bss], w=[br])


def emit_tail(C, get_x, ntiles, xsT_dram=None, out_dram=None, g_bc=None, g_buf=None, ident=None, ident_buf=None):
    S, nc = C.S, C.nc
    mark = len(C.stack)
    junk = C.sb("t_junk", [128, D], BF16)
    bjunk = Buf()
    ss = [C.sb(f"t_ss{i}", [128, 1], F32) for i in range(2)]
    rr = [C.sb(f"t_r{i}", [128, 1], F32) for i in range(2)]
    bss = [Buf(), Buf()]
    brr = [Buf(), Buf()]
    toks = []
    if xsT_dram is not None:
        xs = [C.sb(f"t_xs{i}", [128, D], BF16) for i in range(2)]
        bxs = [Buf(), Buf()]
        pt = [C.ps(f"t_pt{i}", [128, 8, 128], BF16) for i in range(2)]
        bpt = [Buf(), Buf()]
        stage = [C.sb(f"t_st{i}", [128, 8, 512], BF16) for i in range(2)]
        bst = [Buf(), Buf()]
    else:
        xo = [C.sb(f"t_xo{i}", [128, D], F32) for i in range(2)]
        bxo = [Buf(), Buf()]
    for t in range(ntiles):
        xt, bx = get_x(t)
        i = t % 2
        S.op("act", lambda e: e.activation(out=junk[:], in_=xt, func=AF.Square, accum_out=ss[i][:]),
             r=[bx], w=[bjunk, bss[i]])
        emit_rstd(S, ss[i][:], rr[i][:], 128, D, bss[i], brr[i])
        if xsT_dram is not None:
            S.op("dve", lambda e: e.tensor_scalar(out=xs[i][:], in0=xt, scalar1=rr[i][:], scalar2=None, op0=ALU.mult),
                 r=[bx, brr[i]], w=[bxs[i]])
            S.op("pe", [(lambda e, c=c: e.transpose(out=pt[i][:, c, :], in_=xs[i][:, c * 128:(c + 1) * 128], identity=ident))
                        for c in range(8)], r=[bxs[i], ident_buf], w=[bpt[i]])
            g = (t // 4) % 2
            S.op("act", lambda e: e.copy(out=stage[g][:, :, (t % 4) * 128:(t % 4 + 1) * 128], in_=pt[i][:]),
                 r=[bpt[i]], w=[bst[g]])
            if t % 4 == 3:
                c0 = (t // 4) * 512
                toks.append(S.dma("pool", xsT_dram[:, :, c0:c0 + 512].rearrange("c p t -> p c t"), stage[g][:],
                                  r=[bst[g]], slot=f"tst{g}"))
        else:
            S.op("dve", lambda e: e.scalar_tensor_tensor(out=xo[i][:], in0=xt, scalar=rr[i][:], in1=g_bc,
                                                          op0=ALU.mult, op1=ALU.mult),
                 r=[bx, brr[i], g_buf], w=[bxo[i]])
            toks.append(S.dma("pool", out_dram[t * 128:(t + 1) * 128, :], xo[i][:], r=[bxo[i]], slot=f"tout{i}"))
    return toks, mark


def load_ident(C):
    S, nc = C.S, C.nc
    ident = C.sb("ident", [128, 128], BF16)
    b = Buf()
    ones = C.sb("ident_ones", [128, 128], BF16)
    bo = Buf()
    S.op("pool", lambda e: e.memset(ones[:], 1.0), w=[bo])
    S.op("pool", lambda e: e.affine_select(out=ident[:], in_=ones[:], pattern=[[-1, 128]], compare_op=ALU.is_equal,
                                           fill=0.0, base=0, channel_multiplier=1), r=[bo], w=[b])
    return ident, b


def build_phaseN():
    nc = bass.Bass("TRN2", target_bir_lowering=False)
    x = nc.dram_tensor("x", [TOK, D], F32, kind="ExternalInput").ap()
    xsT = nc.dram_tensor("xsT", [8, 128, TOK], BF16, kind="ExternalOutput").ap()
    C = Ctx(nc, "nn")
    S = C.S
    ident, bid = load_ident(C)
    xb = [C.sb(f"xb{i}", [128, D], F32) for i in range(2)]
    bxb = [Buf(), Buf()]

    def get_x(t):
        i = t % 2
        S.dma("sp", xb[i][:], x[t * 128:(t + 1) * 128, :], w=[bxb[i]], slot=f"xl{i}")
        return xb[i][:], bxb[i]

    toks, _ = emit_tail(C, get_x, TOK // 128, xsT_dram=xsT, ident=ident[:], ident_buf=bid)
    S.finish(toks)
    C.release()
    return nc


def prep_weight(C, wsrc, dst, ncols, k, stg, bstg, bdst, scale=None, bscale=None):
    S = C.S
    i = k % 2
    S.dma("sp", stg[i][:, :ncols], wsrc, w=[bstg[i]], slot=f"wst{i}")
    eng = "dve" if k % 2 == 0 else "pool"
    if scale is None:
        S.op(eng, lambda e: e.tensor_copy(out=dst, in_=stg[i][:, :ncols]), r=[bstg[i]], w=[bdst])
    else:
        S.op(eng, lambda e: e.tensor_scalar(out=dst, in0=stg[i][:, :ncols], scalar1=scale, scalar2=None, op0=ALU.mult),
             r=[bstg[i], bscale], w=[bdst])


def emit_phaseB(C, x_in, oT_in, xh_in, oTh_in, w_o, ln2_g, w_up, conv_w, conv_b, w_down, xmid_d, xs2T_d, xnew_d):
    S, nc = C.S, C.nc
    mark0 = len(C.stack)
    ident, bid = load_ident(C)
    stg = [C.sb(f"wstg{i}", [128, 1408], F32) for i in range(2)]
    bstg = [Buf(), Buf()]
    mark1 = len(C.stack)
    wo = C.sb("wo", [128, 8, D], BF16)
    bwo = Buf()
    for kc in range(8):
        prep_weight(C, w_o[kc * 128:(kc + 1) * 128, :], wo[:, kc, :], D, kc, stg, bstg, bwo)
    xb = [C.sb(f"b1x{i}", [128, D], F32) for i in range(2)]
    bxb = [Buf(), Buf()]
    ob = [C.sb(f"b1o{i}", [128, 8, 128], BF16) for i in range(2)]
    bob = [Buf(), Buf()]
    pm = [C.ps(f"b1pm{i}", [128, 2, 512], F32) for i in range(2)]
    bpm = [Buf(), Buf()]
    xm = [C.sb(f"b1xm{i}", [128, D], F32) for i in range(2)]
    bxm = [Buf(), Buf()]
    junk = C.sb("b1junk", [128, D], BF16)
    bjunk = Buf()
    ss = [C.sb(f"b1ss{i}", [128, 1], F32) for i in range(2)]
    rr = [C.sb(f"b1r{i}", [128, 1], F32) for i in range(2)]
    bss = [Buf(), Buf()]
    brr = [Buf(), Buf()]
    xs = [C.sb(f"b1xs{i}", [128, D], BF16) for i in range(2)]
    bxs = [Buf(), Buf()]
    pt = [C.ps(f"b1pt{i}", [128, 8, 128], BF16) for i in range(2)]
    bpt = [Buf(), Buf()]
    st = [C.sb(f"b1st{i}", [128, 8, 128], BF16) for i in range(2)]
    bst = [Buf(), Buf()]
    bxmid_d = Buf()
    bxs2T_d = Buf()
    NT = TOK // 128
    for t in range(-1, NT):
        i = t % 2
        if t < 0:
            S.dma("sp", xb[i][:], xh_in[:, :], w=[bxb[i]], slot=f"b1x{i}")
            S.dma("sp", ob[i][:], oTh_in.rearrange("c p t -> p c t"), w=[bob[i]], slot=f"b1o{i}")
        else:
            S.dma("sp", xb[i][:], x_in[t * 128:(t + 1) * 128, :], w=[bxb[i]], slot=f"b1x{i}")
            S.dma("sp", ob[i][:], oT_in[:, :, t * 128:(t + 1) * 128].rearrange("c p t -> p c t"), w=[bob[i]],
                  slot=f"b1o{i}")
        for h in range(2):
            S.op("pe", [(lambda e, kc=kc, h=h: e.matmul(pm[i][:, h, :], lhsT=ob[i][:, kc, :],
                                                        rhs=wo[:, kc, h * 512:(h + 1) * 512],
                                                        start=(kc == 0), stop=(kc == 7))) for kc in range(8)],
                 r=[bob[i], bwo], w=[bpm[i]])
        S.op("dve", lambda e: e.tensor_tensor(out=xm[i][:], in0=pm[i][:].rearrange("p a b -> p (a b)"), in1=xb[i][:],
                                              op=ALU.add), r=[bpm[i], bxb[i]], w=[bxm[i]])
        if t >= 0:
            S.dma("pool", xmid_d[t * 128:(t + 1) * 128, :], xm[i][:], r=[bxm[i]], w=[bxmid_d], slot=f"b1xm{i}")
        S.op("act", lambda e: e.activation(out=junk[:], in_=xm[i][:], func=AF.Square, accum_out=ss[i][:]),
             r=[bxm[i]], w=[bjunk, bss[i]])
        emit_rstd(S, ss[i][:], rr[i][:], 128, D, bss[i], brr[i])
        S.op("dve", lambda e: e.tensor_scalar(out=xs[i][:], in0=xm[i][:], scalar1=rr[i][:], scalar2=None, op0=ALU.mult),
             r=[bxm[i], brr[i]], w=[bxs[i]])
        S.op("pe", [(lambda e, c=c: e.transpose(out=pt[i][:, c, :], in_=xs[i][:, c * 128:(c + 1) * 128], identity=ident[:]))
                    for c in range(8)], r=[bxs[i], bid], w=[bpt[i]])
        S.op("act", lambda e: e.copy(out=st[i][:], in_=pt[i][:]), r=[bpt[i]], w=[bst[i]])
        c0 = (t + 1) * 128
        S.dma("pool", xs2T_d[:, :, c0:c0 + 128].rearrange("c p t -> p c t"), st[i][:], r=[bst[i]], w=[bxs2T_d],
              slot=f"b1st{i}")
    C.release(mark1)
    S.barrier()
    g2 = C.sb("ln2g", [128, 8], F32)
    bg2 = Buf()
    S.dma("sp", g2[:], ln2_g[:, :], w=[bg2], slot="cst")
    cw = C.sb("convw", [128, 3, 44], F32)
    cb = C.sb("convb", [128, 44], F32)
    bcw = Buf()
    S.dma("sp", cw[:], conv_w[:, :, :], w=[bcw], slot="cst")
    S.dma("sp", cb[:], conv_b[:, :], w=[bcw], slot="cst")
    wup = C.sb("wup", [128, 8, 2 * DFF], BF16)
    bwup = Buf()
    k = 0
    for kc in range(8):
        for h in range(4):
            prep_weight(C, w_up[kc * 128:(kc + 1) * 128, h * 1408:(h + 1) * 1408], wup[:, kc, h * 1408:(h + 1) * 1408], 1408,
                        k, stg, bstg, bwup, scale=g2[:, kc:kc + 1], bscale=bg2)
            k += 1
    wdn = C.sb("wdn", [128, 22, D], BF16)
    bwdn = Buf()
    for fc in range(0, 22, 2):
        for a in range(2):
            prep_weight(C, w_down[(fc + a) * 128:(fc + a + 1) * 128, :], wdn[:, fc + a, :], D, k, stg, bstg, bwdn)
            k += 1
    xg = [C.sb(f"b2xg{i}", [128, 8, 512], BF16) for i in range(2)]
    xh = [C.sb(f"b2xh{i}", [128, 8, 2], BF16) for i in range(2)]
    bxg = [Buf(), Buf()]
    pg = [C.ps(f"b2pg{i}", [128, 512], F32) for i in range(2)]
    pu = [C.ps(f"b2pu{i}", [128, 512], F32) for i in range(2)]
    ph = [C.ps(f"b2ph{i}", [128, 4], F32) for i in range(2)]
    bpg = [Buf(), Buf()]
    bpu = [Buf(), Buf()]
    bph = [Buf(), Buf()]
    yg = [C.sb(f"b2yg{i}", [128, 512], F32) for i in range(2)]
    yu = [C.sb(f"b2yu{i}", [128, 512], F32) for i in range(2)]
    hs = [C.sb(f"b2hs{i}", [128, 4], F32) for i in range(2)]
    byg = [Buf(), Buf()]
    byu = [Buf(), Buf()]
    bhs = [Buf(), Buf()]
    sg = [C.sb(f"b2sg{i}", [128, 512], F32) for i in range(2)]
    bsg = [Buf(), Buf()]
    hT = C.sb("b2hT", [128, 22, 512], BF16)
    bhT = [Buf() for _ in range(22)]
    pd = C.ps("b2pd", [128, 2, 512], F32)
    bpd = Buf()
    xmb = [C.sb(f"b2xm{i}", [128, D], F32) for i in range(2)]
    bxmb = [Buf(), Buf()]
    xnb = xmb
    bxnb = bxmb
    bxnew_d = Buf()
    cnt = 0
    for gi in range(TOK // 512):
        i = gi % 2
        c0 = 128 + gi * 512
        S.dma("sp", xg[i][:], xs2T_d[:, :, c0:c0 + 512].rearrange("c p t -> p c t"), r=[bxs2T_d], w=[bxg[i]],
              slot=f"b2xg{i}")
        S.dma("sp", xh[i][:], xs2T_d[:, :, c0 - 2:c0].rearrange("c p t -> p c t"), r=[bxs2T_d], w=[bxg[i]],
              slot=f"b2xh{i}")
        for fc in range(22):
            j = cnt % 2
            cnt += 1
            for (pp, bpp, col0) in ((pg, bpg, fc * 128), (pu, bpu, DFF + fc * 128)):
                S.op("pe", [(lambda e, kc=kc, pp=pp, col0=col0: e.matmul(pp[j][:], lhsT=wup[:, kc, col0:col0 + 128],
                                                                        rhs=xg[i][:, kc, :], start=(kc == 0),
                                                                        stop=(kc == 7))) for kc in range(8)],
                     r=[bwup, bxg[i]], w=[bpp[j]])
            for hh, col0 in ((0, fc * 128), (1, DFF + fc * 128)):
                S.op("pe", [(lambda e, kc=kc, hh=hh, col0=col0: e.matmul(ph[j][:, hh * 2:hh * 2 + 2],
                                                                        lhsT=wup[:, kc, col0:col0 + 128],
                                                                        rhs=xh[i][:, kc, :], start=(kc == 0),
                                                                        stop=(kc == 7))) for kc in range(8)],
                     r=[bwup, bxg[i]], w=[bph[j]])
            S.op("act", lambda e: e.copy(out=hs[j][:], in_=ph[j][:]), r=[bph[j]], w=[bhs[j]])
            for (pp, bpp, yy, byy, ch, hoff) in ((pg, bpg, yg, byg, fc, 0), (pu, bpu, yu, byu, 22 + fc, 2)):
                w0 = cw[:, 0, ch:ch + 1]
                w1 = cw[:, 1, ch:ch + 1]
                w2 = cw[:, 2, ch:ch + 1]
                S.op("act", lambda e: e.activation(out=yy[j][:], in_=pp[j][:], func=AF.Identity, bias=cb[:, ch:ch + 1],
                                                   scale=w2), r=[bpp[j], bcw], w=[byy[j]])
                S.op("dve", lambda e: e.scalar_tensor_tensor(out=yy[j][:, 1:512], in0=pp[j][:, 0:511], scalar=w1,
                                                              in1=yy[j][:, 1:512], op0=ALU.mult, op1=ALU.add),
                     r=[bpp[j], bcw, byy[j]], w=[byy[j]])
                S.op("dve", lambda e: e.scalar_tensor_tensor(out=yy[j][:, 2:512], in0=pp[j][:, 0:510], scalar=w0,
                                                              in1=yy[j][:, 2:512], op0=ALU.mult, op1=ALU.add),
                     r=[bpp[j], bcw, byy[j]], w=[byy[j]])
                S.op("dve", lambda e: e.scalar_tensor_tensor(out=yy[j][:, 0:2], in0=hs[j][:, hoff:hoff + 2], scalar=w0,
                                                              in1=yy[j][:, 0:2], op0=ALU.mult, op1=ALU.add),
                     r=[bhs[j], bcw, byy[j]], w=[byy[j]])
                S.op("dve", lambda e: e.scalar_tensor_tensor(out=yy[j][:, 0:1], in0=hs[j][:, hoff + 1:hoff + 2], scalar=w1,
                                                              in1=yy[j][:, 0:1], op0=ALU.mult, op1=ALU.add),
                     r=[bhs[j], bcw, byy[j]], w=[byy[j]])
            S.op("act", lambda e: e.activation(out=sg[j][:], in_=yg[j][:], func=AF.Silu), r=[byg[j]], w=[bsg[j]])
            S.op("pool", lambda e: e.tensor_tensor(out=hT[:, fc, :], in0=sg[j][:], in1=yu[j][:], op=ALU.mult),
                 r=[bsg[j], byu[j]], w=[bhT[fc]])
        for tt in range(4):
            t = gi * 4 + tt
            m = t % 2
            S.dma("sp", xmb[m][:], xmid_d[t * 128:(t + 1) * 128, :], r=[bxmid_d], w=[bxmb[m]], slot=f"b2xm{m}")
            for h in range(2):
                S.op("pe", [(lambda e, fc=fc, h=h: e.matmul(pd[:, h, :], lhsT=hT[:, fc, tt * 128:(tt + 1) * 128],
                                                            rhs=wdn[:, fc, h * 512:(h + 1) * 512], start=(fc == 0),
                                                            stop=(fc == 21))) for fc in range(22)],
                     r=bhT + [bwdn], w=[bpd])
            S.op("dve", lambda e: e.tensor_tensor(out=xnb[m][:], in0=pd[:].rearrange("p a b -> p (a b)"), in1=xmb[m][:],
                                                  op=ALU.add), r=[bpd], w=[bxnb[m]])
            S.dma("pool", xnew_d[t * 128:(t + 1) * 128, :], xnb[m][:], r=[bxnb[m]], w=[bxnew_d], slot=f"b2xn{m}")
    C.release(mark0)
    S.barrier()
    return bxnew_d


def build_phaseB(last):
    nc = bass.Bass("TRN2", target_bir_lowering=False)
    dt = nc.dram_tensor
    x_in = dt("x", [TOK, D], F32, kind="ExternalInput").ap()
    oT_in = dt("oT", [8, 128, TOK], BF16, kind="ExternalInput").ap()
    xh_in = dt("xh", [128, D], F32, kind="ExternalInput").ap()
    oTh_in = dt("oTh", [8, 128, 128], BF16, kind="ExternalInput").ap()
    w_o = dt("w_o", [D, D], F32, kind="ExternalInput").ap()
    ln2_g = dt("ln2_g", [128, 8], F32, kind="ExternalInput").ap()
    w_up = dt("w_up", [D, 2 * DFF], F32, kind="ExternalInput").ap()
    conv_w = dt("conv_w", [128, 3, 44], F32, kind="ExternalInput").ap()
    conv_b = dt("conv_b", [128, 44], F32, kind="ExternalInput").ap()
    w_down = dt("w_down", [DFF, D], F32, kind="ExternalInput").ap()
    xmid_d = dt("xmid_s", [TOK, D], F32, kind="Internal").ap()
    xs2T_d = dt("xs2T_s", [8, 128, 128 + TOK], BF16, kind="Internal").ap()
    C = Ctx(nc, "b")
    S = C.S
    if last:
        fg = dt("final_g", [D], F32, kind="ExternalInput").ap()
        xnew_d = dt("xnew_s", [TOK, D], F32, kind="Internal").ap()
        out_d = dt("out", [TOK, D], F32, kind="ExternalOutput").ap()
    else:
        xnew_d = dt("xnew", [TOK, D], F32, kind="ExternalOutput").ap()
        xsT_d = dt("xsT", [8, 128, TOK], BF16, kind="ExternalOutput").ap()
    bxn = emit_phaseB(C, x_in, oT_in, xh_in, oTh_in, w_o, ln2_g, w_up, conv_w, conv_b, w_down, xmid_d, xs2T_d, xnew_d)
    ident, bid = load_ident(C)
    xb = [C.sb(f"xb{i}", [128, D], F32) for i in range(2)]
    bxb = [Buf(), Buf()]

    def get_x(t):
        i = t % 2
        S.dma("sp", xb[i][:], xnew_d[t * 128:(t + 1) * 128, :], r=[bxn], w=[bxb[i]], slot=f"xl{i}")
        return xb[i][:], bxb[i]

    if last:
        gbc = C.sb("gbc", [128, D], F32)
        bg = Buf()
        S.dma("sp", gbc[:], fg.partition_broadcast(128), w=[bg], slot="cst2")
        toks, _ = emit_tail(C, get_x, TOK // 128, out_dram=out_d, g_bc=gbc[:], g_buf=bg)
    else:
        toks, _ = emit_tail(C, get_x, TOK // 128, xsT_dram=xsT_d, ident=ident[:], ident_buf=bid)
    S.barrier()
    C.release()
    return nc


FQ, FK, DQ, DK, CQ, CKV, KR, FILL, KRS, FV, FF, DV = 0, 64, 128, 256, 384, 640, 768, 800, 864, 896, 960, 961
NCOL = 1089


def emit_phaseA(C, SL, layer, xsT, w_sel, ln1g, wuq, qng, wukv, kvng, dng, fb, lamv, cosd, sind,
                qaugd, kaugd, biasd, tdd, tfd, tmd, trid, oT_d):
    S, nc = C.S, C.nc
    NCH = SL // 512
    NBLK = SL // 128
    lam_init = 0.8 - 0.6 * float(np.exp(-0.3 * layer))
    mark0 = len(C.stack)
    cst = {}

    def cload(name, src, shape, dt, q="sp"):
        t = C.sb(name, shape, dt)
        b = Buf()
        S.dma(q, t[:], src, w=[b], slot="cA")
        cst[name] = (t, b)
        return t, b

    g1, bg1 = cload("ln1g", ln1g[:, :], [128, 8], F32)
    qg, bqg = cload("qng", qng[:, :], [128, 2], F32)
    kvg, bkvg = cload("kvng", kvng[:, :], [128, 1], F32)
    dg, bdg = cload("dng", dng[:, :], [128, 1], F32)
    fbt, bfb = cload("fb", fb[:, :], [128, 1], F32)
    lmv, blmv = cload("lamv", lamv.partition_broadcast(128), [128, 256], F32)
    qaug, bqaug = cload("qaugd", qaugd[:, :], [128, 512], BF16)
    kaug, bkaug = cload("kaugd", kaugd[:, :], [128, 128], BF16)
    bsd, bbsd = cload("biasd", biasd[:, :], [128, 64], F32)
    TD, bTD = cload("tdd", tdd[:, :], [128, 128], F32)
    TF, bTF = cload("tfd", tfd[:, :], [128, 128], F32)
    TM, bTM = cload("tmd", tmd[:, :], [128, 128], F32)
    tri, btri = cload("trid", trid[:, :], [128, 128], F32)
    ones_bf = C.sb("ones_bf", [128, 128], BF16)
    ones_f = C.sb("ones_f", [128, 128], F32)
    zer_bf = C.sb("zer_bf", [128, 128], BF16)
    bones = Buf()
    S.op("pool", lambda e: e.memset(ones_bf[:], 1.0), w=[bones])
    S.op("pool", lambda e: e.memset(ones_f[:], 1.0), w=[bones])
    S.op("pool", lambda e: e.memset(zer_bf[:], 0.0), w=[bones])
    kaugF = C.sb("kaugF", [64, 128], BF16)
    bkaugF = Buf()
    S.op("pool", lambda e: e.memset(kaugF[:], 0.0), w=[bkaugF])
    S.op("pool", lambda e: e.memset(kaugF[0:1, :], 1.0), w=[bkaugF])
    S.op("pool", lambda e: e.memset(kaugF[32:33, :], 1.0), w=[bkaugF])
    nfb = C.sb("nfb", [128, 1], F32)
    bnfb = Buf()
    S.op("dve", lambda e: e.tensor_scalar(out=nfb[:], in0=fbt[:], scalar1=-1.0, scalar2=None, op0=ALU.mult), r=[bfb], w=[bnfb])
    dgs = C.sb("dgs", [128, 1], F32)
    bdgs = Buf()
    S.op("dve", lambda e: e.tensor_scalar(out=dgs[:], in0=dg[:], scalar1=float(1.0 - lam_init), scalar2=None, op0=ALU.mult),
         r=[bdg], w=[bdgs])
    lprod = C.sb("lprod", [128, 2, 64], F32)
    ldot = C.sb("ldot", [128, 2], F32)
    nlam = C.sb("nlam", [128, 1], F32)
    blam = Buf()
    lm3 = lmv[:].rearrange("p (a d) -> p a d", a=4)
    S.op("dve", lambda e: e.tensor_tensor(out=lprod[:, 0, :], in0=lm3[:, 0, :], in1=lm3[:, 1, :], op=ALU.mult), r=[blmv], w=[blam])
    S.op("dve", lambda e: e.tensor_tensor(out=lprod[:, 1, :], in0=lm3[:, 2, :], in1=lm3[:, 3, :], op=ALU.mult), r=[blmv, blam], w=[blam])
    S.op("dve", lambda e: e.tensor_reduce(out=ldot[:], in_=lprod[:], axis=AX.X, op=ALU.add), r=[blam], w=[blam])
    S.op("act", lambda e: e.activation(out=ldot[:], in_=ldot[:], func=AF.Exp), r=[blam], w=[blam])
    S.op("dve", lambda e: e.tensor_tensor(out=nlam[:], in0=ldot[:, 1:2], in1=ldot[:, 0:1], op=ALU.subtract), r=[blam], w=[blam])
    S.op("dve", lambda e: e.tensor_scalar(out=nlam[:], in0=nlam[:], scalar1=float(-lam_init), scalar2=None, op0=ALU.add),
         r=[blam], w=[blam])
    stg = [C.sb(f"astg{i}", [128, NCOL], F32) for i in range(2)]
    bstg = [Buf(), Buf()]
    wsel = C.sb("wsel", [128, 8, NCOL], BF16)
    bwsel = Buf()
    for kc in range(8):
        prep_weight(C, w_sel[kc * 128:(kc + 1) * 128, :], wsel[:, kc, :], NCOL, kc, stg, bstg, bwsel,
                    scale=g1[:, kc:kc + 1], bscale=bg1)
    wq = C.sb("wuq", [128, 2, 192], BF16)
    bwq = Buf()
    qgs = C.sb("qgs", [128, 2], F32)
    bqgs = Buf()
    S.op("dve", lambda e: e.tensor_scalar(out=qgs[:], in0=qg[:], scalar1=float(96 ** -0.5), scalar2=None, op0=ALU.mult),
         r=[bqg], w=[bqgs])
    for a in range(2):
        prep_weight(C, wuq[a * 128:(a + 1) * 128, :], wq[:, a, :], 192, a, stg, bstg, bwq, scale=qgs[:, a:a + 1], bscale=bqgs)
    wkv = C.sb("wukv", [128, 128], BF16)
    bwkv = Buf()
    prep_weight(C, wukv[:, :], wkv[:], 128, 0, stg, bstg, bwkv, scale=kvg[:, 0:1], bscale=bkvg)
    QTf = C.sb("QTf", [64, SL], BF16)
    KTf = C.sb("KTf", [64, SL], BF16)
    Vf = C.sb("Vf", [128, NBLK, 65], BF16)
    Q12 = C.sb("Q12", [128, SL], BF16)
    K12 = C.sb("K12", [128, SL], BF16)
    Vd = C.sb("Vd", [128, NBLK, 128], BF16)
    QTm = C.sb("QTm", [96, SL], BF16)
    KTm = C.sb("KTm", [96, SL], BF16)
    Vm = C.sb("Vm", [128, NBLK, 65], BF16)
    Fcol = C.sb("Fcol", [128, NBLK], F32)
    bqkv = [Buf() for _ in range(NCH)]
    bvone = Buf()
    S.op("pool", lambda e: e.memset(Vf[:, :, 64:65], 1.0), w=[bvone])
    S.op("pool", lambda e: e.memset(Vm[:, :, 64:65], 1.0), w=[bvone])
    mark1 = len(C.stack)
    xc = [C.sb(f"xc{i}", [128, 8, 512], BF16) for i in range(2)]
    bxc = [Buf(), Buf()]
    pf = [C.ps(f"pf{i}", [128, 512], F32) for i in range(3)]
    bpf = [Buf() for _ in range(3)]
    pv = [C.ps(f"pv{i}", [128, 2, 256], F32) for i in range(2)]
    bpv = [Buf(), Buf()]
    psm = C.ps("psm", [128, 8], F32)
    bpsm = Buf()
    pvm = C.ps("pvm", [128, 4, 64], F32)
    bpvm = Buf()
    cqT = C.sb("cqT", [128, 2, 512], BF16)
    sqq = C.sb("sqq", [128, 2, 512], BF16)
    bcq = Buf()
    ckvT = C.sb("ckvT", [128, 512], BF16)
    sqkv = C.sb("sqkv", [128, 512], BF16)
    bckv = Buf()
    rq = C.sb("rq", [96, 512], F32)
    brq = Buf()
    rkv = C.sb("rkv", [64, 512], F32)
    brkv = Buf()
    rkc = C.sb("rkc", [128, 4], F32)
    brkc = Buf()
    cosT = [C.sb(f"cosT{i}", [96, 512], F32) for i in range(2)]
    sinT = [C.sb(f"sinT{i}", [96, 512], F32) for i in range(2)]
    bcs = [Buf(), Buf()]
    t1 = C.sb("t1", [96, 512], F32)
    t2 = C.sb("t2", [96, 512], F32)
    bt1 = Buf()
    bt2 = Buf()
    npf = [0]

    def fgroup(cols, M, xi, kcs=8):
        i = npf[0] % 3
        npf[0] += 1
        S.op("pe", [(lambda e, kc=kc: e.matmul(pf[i][0:M, :], lhsT=wsel[:, kc, cols:cols + M], rhs=xc[xi][:, kc, :],
                                               start=(kc == 0), stop=(kc == 7))) for kc in range(8)],
             r=[bwsel, bxc[xi]], w=[bpf[i]])
        return pf[i], bpf[i]

    def rstd_from_ps(p, bp, M, n, out, bout):
        S.op("act", lambda e: e.activation(out=out, in_=p, func=AF.Ln, bias=float(EPS), scale=1.0 / n), r=[bp], w=[bout])
        S.op("act", lambda e: e.activation(out=out, in_=out, func=AF.Exp, scale=-0.5), r=[bout], w=[bout])

    for c in range(NCH):
        xi = c % 2
        t0 = c * 512
        bo = bqkv[c]
        S.dma("sp", xc[xi][:], xsT[:, :, t0:t0 + 512].rearrange("c p t -> p c t"), w=[bxc[xi]], slot=f"xc{xi}")
        S.dma("sp", cosT[xi][64:96, :], cosd[:, t0:t0 + 512], w=[bcs[xi]], slot=f"cs{xi}")
        S.dma("sp", sinT[xi][64:96, :], sind[:, t0:t0 + 512], w=[bcs[xi]], slot=f"cs{xi}")
        p, bp = fgroup(FQ, 64, xi)
        S.op("act", lambda e: e.activation(out=QTf[:, t0:t0 + 512], in_=p[0:64, :], func=AF.Copy, scale=0.125), r=[bp], w=[bo])
        p, bp = fgroup(FK, 64, xi)
        S.op("dve", lambda e: e.tensor_copy(out=KTf[:, t0:t0 + 512], in_=p[0:64, :]), r=[bp], w=[bo])
        p, bp = fgroup(DQ, 128, xi)
        S.op("act", lambda e: e.activation(out=Q12[:, t0:t0 + 512], in_=p[:, :], func=AF.Copy, scale=0.125), r=[bp], w=[bo])
        p, bp = fgroup(DK, 128, xi)
        S.op("dve", lambda e: e.tensor_copy(out=K12[:, t0:t0 + 512], in_=p[:, :]), r=[bp], w=[bo])
        for a in range(2):
            p, bp = fgroup(CQ + a * 128, 128, xi)
            S.op("dve", lambda e: e.tensor_copy(out=cqT[:, a, :], in_=p[:, :]), r=[bp], w=[bcq])
            S.op("act", lambda e: e.activation(out=sqq[:, a, :], in_=p[:, :], func=AF.Square), r=[bp], w=[bcq])
        i = npf[0] % 3
        npf[0] += 1
        S.op("pe", [(lambda e, a=a: e.matmul(pf[i][0:96, :], lhsT=ones_bf[:, 0:96], rhs=sqq[:, a, :], start=(a == 0),
                                             stop=(a == 1))) for a in range(2)], r=[bones, bcq], w=[bpf[i]])
        rstd_from_ps(pf[i][0:96, :], bpf[i], 96, 256, rq[:], brq)
        i1 = npf[0] % 3
        npf[0] += 1
        S.op("pe", [(lambda e, a=a: e.matmul(pf[i1][0:96, :], lhsT=wq[:, a, 0:96], rhs=cqT[:, a, :], start=(a == 0),
                                             stop=(a == 1))) for a in range(2)], r=[bwq, bcq], w=[bpf[i1]])
        i2 = npf[0] % 3
        npf[0] += 1
        S.op("pe", [(lambda e, a=a: e.matmul(pf[i2][0:96, :], lhsT=wq[:, a, 96:192], rhs=cqT[:, a, :], start=(a == 0),
                                             stop=(a == 1))) for a in range(2)], r=[bwq, bcq], w=[bpf[i2]])
        S.op("dve", lambda e: e.tensor_tensor(out=QTm[0:64, t0:t0 + 512], in0=pf[i1][0:64, :], in1=rq[0:64, :], op=ALU.mult),
             r=[bpf[i1], brq], w=[bo])
        S.op("dve", lambda e: e.tensor_tensor(out=t1[64:96, :], in0=pf[i1][64:96, :], in1=cosT[xi][64:96, :], op=ALU.mult),
             r=[bpf[i1], bcs[xi]], w=[bt1])
        S.op("dve", lambda e: e.tensor_tensor(out=t2[64:96, :], in0=pf[i2][64:96, :], in1=sinT[xi][64:96, :], op=ALU.mult),
             r=[bpf[i2], bcs[xi]], w=[bt2])
        S.op("pool", lambda e: e.tensor_tensor(out=t1[64:96, :], in0=t1[64:96, :], in1=t2[64:96, :], op=ALU.add),
             r=[bt1, bt2], w=[bt1])
        S.op("pool", lambda e: e.tensor_tensor(out=QTm[64:96, t0:t0 + 512], in0=t1[64:96, :], in1=rq[64:96, :], op=ALU.mult),
             r=[bt1, brq], w=[bo])
        p, bp = fgroup(CKV, 128, xi)
        S.op("dve", lambda e: e.tensor_copy(out=ckvT[:], in_=p[:, :]), r=[bp], w=[bckv])
        S.op("act", lambda e: e.activation(out=sqkv[:], in_=p[:, :], func=AF.Square), r=[bp], w=[bckv])
        i = npf[0] % 3
        npf[0] += 1
        S.op("pe", lambda e: e.matmul(pf[i][0:64, :], lhsT=ones_bf[:, 0:64], rhs=sqkv[:], start=True, stop=True),
             r=[bones, bckv], w=[bpf[i]])
        rstd_from_ps(pf[i][0:64, :], bpf[i], 64, 128, rkv[:], brkv)
        i = npf[0] % 3
        npf[0] += 1
        S.op("pe", lambda e: e.matmul(pf[i][0:64, :], lhsT=wkv[:, 0:64], rhs=ckvT[:], start=True, stop=True),
             r=[bwkv, bckv], w=[bpf[i]])
        S.op("dve", lambda e: e.tensor_tensor(out=KTm[0:64, t0:t0 + 512], in0=pf[i][0:64, :], in1=rkv[:], op=ALU.mult),
             r=[bpf[i], brkv], w=[bo])
        S.op("pe", [(lambda e, tb=tb: e.matmul(psm[:, tb:tb + 1], lhsT=sqkv[:, tb * 128:(tb + 1) * 128], rhs=ones_bf[:, 0:1],
                                               start=True, stop=True)) for tb in range(4)], r=[bones, bckv], w=[bpsm])
        rstd_from_ps(psm[:, 0:4], bpsm, 128, 128, rkc[:], brkc)
        S.op("pe", [(lambda e, tb=tb: e.matmul(pvm[:, tb, :], lhsT=ckvT[:, tb * 128:(tb + 1) * 128], rhs=wkv[:, 64:128],
                                               start=True, stop=True)) for tb in range(4)], r=[bwkv, bckv], w=[bpvm])
        for tb in range(4):
            S.op("act", lambda e: e.activation(out=Vm[:, c * 4 + tb, 0:64], in_=pvm[:, tb, :], func=AF.Copy,
                                               scale=rkc[:, tb:tb + 1]), r=[bpvm, brkc], w=[bo])
        p, bp = fgroup(CKV + 64, 96, xi)
        p2, bp2 = fgroup(FILL, 96, xi)
        S.op("dve", lambda e: e.tensor_tensor(out=t1[64:96, :], in0=p[64:96, :], in1=cosT[xi][64:96, :], op=ALU.mult),
             r=[bp, bcs[xi]], w=[bt1])
        S.op("dve", lambda e: e.tensor_tensor(out=t2[64:96, :], in0=p2[64:96, :], in1=sinT[xi][64:96, :], op=ALU.mult),
             r=[bp2, bcs[xi]], w=[bt2])
        S.op("pool", lambda e: e.tensor_tensor(out=KTm[64:96, t0:t0 + 512], in0=t1[64:96, :], in1=t2[64:96, :], op=ALU.add),
             r=[bt1, bt2], w=[bo])
        for tb in range(4):
            k = tb // 2
            S.op("pe", [(lambda e, kc=kc: e.matmul(pv[k][:, tb % 2, 0:193], lhsT=xc[xi][:, kc, tb * 128:(tb + 1) * 128],
                                                   rhs=wsel[:, kc, FV:FV + 193], start=(kc == 0), stop=(kc == 7)))
                        for kc in range(8)], r=[bwsel, bxc[xi]], w=[bpv[k]])
            blk = c * 4 + tb
            S.op("act", lambda e: e.copy(out=Vf[:, blk, 0:64], in_=pv[k][:, tb % 2, 0:64]), r=[bpv[k]], w=[bo])
            S.op("dve", lambda e: e.tensor_copy(out=Fcol[:, blk:blk + 1], in_=pv[k][:, tb % 2, 64:65]), r=[bpv[k]], w=[bo])
            S.op("dve", lambda e: e.tensor_copy(out=Vd[:, blk, :], in_=pv[k][:, tb % 2, 65:193]), r=[bpv[k]], w=[bo])
    C.release(mark1)
    S.barrier()
    ball = Buf()
    L = C.sb("Lg", [128, NBLK], F32)
    bL = Buf()
    S.op("act", lambda e: e.activation(out=L[:], in_=Fcol[:], func=AF.Exp, bias=nfb[:], scale=-1.0), w=[bL])
    S.op("act", lambda e: e.activation(out=L[:], in_=L[:], func=AF.Ln, bias=1.0), r=[bL], w=[bL])
    pbc = C.ps("pbc", [128, 512], F32)
    bpbc = Buf()
    pcl = pbc[:, 0:2 * NBLK].rearrange("p (a n) -> p a n", a=2)
    bpcl = bpbc
    S.op("pe", lambda e: e.matmul(pcl[:, 0, :], lhsT=tri[:], rhs=L[:], start=True, stop=True), r=[bL], w=[bpcl])
    S.op("pe", lambda e: e.matmul(pcl[:, 1, :], lhsT=ones_f[:], rhs=L[:], start=True, stop=True), r=[bL, bpcl], w=[bpcl])
    tot = C.sb("tot", [128, NBLK], F32)
    incl = C.sb("incl", [128, NBLK], F32)
    cumL = C.sb("cumL", [128, NBLK], F32)
    bcum = Buf()
    S.op("dve", lambda e: e.tensor_copy(out=tot[:], in_=pcl[:, 1, :]), r=[bpcl], w=[bcum])
    S.op("dve", lambda e: e.tensor_tensor_scan(out=incl[:], data0=ones_f[:, 0:NBLK], data1=tot[:], initial=0.0,
                                               op0=ALU.mult, op1=ALU.add), r=[bcum], w=[bcum])
    S.op("dve", lambda e: e.tensor_tensor(out=cumL[:], in0=pcl[:, 0, :], in1=incl[:], op=ALU.add), r=[bpcl, bcum], w=[bcum])
    S.op("dve", lambda e: e.tensor_tensor(out=cumL[:], in0=cumL[:], in1=tot[:], op=ALU.subtract), r=[bcum], w=[bcum])
    tabF = C.sb("tabF", [128, NCH, NBLK], F32)
    tabFd = C.sb("tabFd", [128, NBLK], F32)
    for g in range(NCH):
        S.op("dve", lambda e: e.tensor_scalar(out=tabF[:, g, :], in0=cumL[:], scalar1=incl[:, 4 * g + 3:4 * g + 4], scalar2=None,
                                              op0=ALU.subtract), r=[bcum], w=[bcum])
    S.op("dve", lambda e: e.tensor_tensor(out=tabFd[:], in0=cumL[:], in1=incl[:], op=ALU.subtract), r=[bcum], w=[bcum])
    dvl = C.sb("dvl", [33, NBLK], F32)
    dhi = C.sb("dhi", [33, NBLK], BF16)
    dhf = C.sb("dhf", [33, NBLK], F32)
    dlo = C.sb("dlo", [33, NBLK], BF16)
    i4 = incl[0:33, :].rearrange("p (g m) -> p g m", m=4)
    d4 = dvl[:].rearrange("p (g m) -> p g m", m=4)
    for m in range(4):
        S.op("dve", lambda e: e.tensor_tensor(out=d4[:, :, m], in0=i4[:, :, 3], in1=i4[:, :, m], op=ALU.subtract), r=[bcum], w=[bcum])
    S.op("dve", lambda e: e.tensor_copy(out=dhi[:], in_=dvl[:]), r=[bcum], w=[bcum])
    S.op("dve", lambda e: e.tensor_copy(out=dhf[:], in_=dhi[:]), r=[bcum], w=[bcum])
    S.op("dve", lambda e: e.tensor_tensor(out=dhf[:], in0=dvl[:], in1=dhf[:], op=ALU.subtract), r=[bcum], w=[bcum])
    S.op("dve", lambda e: e.tensor_copy(out=dlo[:], in_=dhf[:]), r=[bcum], w=[bcum])
    S.barrier()
    ps_ = [C.ps(f"ps{i}", [128, 512], F32) for i in range(2)]
    bps = [Buf(), Buf()]
    pdg = C.ps("pdg", [128, 128], F32)
    bpdg = Buf()
    pacc = [C.ps(f"pacc{i}", [128, 512], F32) for i in range(2)]
    bpacc = [Buf(), Buf()]
    pden = [C.ps(f"pden{i}", [128, 512], F32) for i in range(2)]
    bpden = [Buf(), Buf()]
    pT = [C.sb(f"pT{i}", [128, 512], BF16) for i in range(2)]
    bpT = [Buf(), Buf()]
    pTd = C.sb("pTd", [128, 128], BF16)
    bpTd = Buf()
    tdg = C.sb("tdg", [128, 128], F32)
    btdg = Buf()
    qaF = [C.sb(f"qaF{i}", [64, 512], BF16) for i in range(2)]
    bqaF = [Buf(), Buf()]
    for i in range(2):
        S.op("pool", lambda e: e.memset(qaF[i][:], 0.0), w=[bqaF[i]])
    rec = C.sb("rec", [128, 512], F32)
    brec = Buf()
    osb = [C.sb(f"osb{i}", [128, 512], F32) for i in range(2)]
    bosb = [Buf(), Buf()]
    o1n = C.sb("o1n", [128, 512], F32)
    bo1n = Buf()
    od = C.sb("od", [128, 512], F32)
    bod = Buf()
    sqd = C.sb("sqd", [128, 512], BF16)
    bsqd = Buf()
    rsd = C.sb("rsd", [128, 512], F32)
    brsd = Buf()
    ost = [C.sb(f"ost{i}", [128, 512], BF16) for i in range(2)]
    bost = [Buf(), Buf()]
    nt = [0]
    nacc = [0]
    nst = [0]
    toks = []

    def attn_group(g, KT, QT, aug, bias_off, bias_diag, T, bT, Vst, Mv, den_sep):
        ai = nacc[0] % 2
        nacc[0] += 1
        acc, bacc = pacc[ai], bpacc[ai]
        dn, bdn = pden[ai], bpden[ai]
        nj = 4 * g + 4
        first = [True]
        q0 = g * 512

        def pv_mm(rhs_ap, c0, c1, rbufs, last):
            fns = [lambda e: e.matmul(acc[0:Mv, c0:c1], lhsT=Vst(j), rhs=rhs_ap, start=first[0], stop=last)]
            wl = [bacc]
            if den_sep:
                fns.append(lambda e: e.matmul(dn[0:1, c0:c1], lhsT=ones_bf[:, 0:1], rhs=rhs_ap, start=first[0], stop=last))
                wl.append(bdn)
            S.op("pe", fns, r=rbufs + [ball, bones], w=wl)
            first[0] = False

        for j in range(nj):
            m = j - 4 * g
            c0 = 0 if m < 0 else 128 * (m + 1)
            if c0 < 512:
                si = nt[0] % 2
                nt[0] += 1
                fns = [lambda e: e.matmul(ps_[si][:, c0:512], lhsT=KT(j), rhs=QT(q0 + c0, q0 + 512), start=True, stop=(aug is None))]
                rb = [ball]
                if aug is not None:
                    ka, qa, bq = aug(g, j, c0)
                    fns.append(lambda e: e.matmul(ps_[si][:, c0:512], lhsT=ka, rhs=qa, start=False, stop=True))
                    rb = rb + bq
                S.op("pe", fns, r=rb, w=[bps[si]])
                S.op("act", lambda e: e.activation(out=pT[si][:, c0:512], in_=ps_[si][:, c0:512], func=AF.Exp,
                                                   bias=bias_off(g, j)), r=[bps[si], ball], w=[bpT[si]])
                pv_mm(pT[si][:, c0:512], c0, 512, [bpT[si]], False)
            if m >= 0:
                S.op("pe", lambda e: e.matmul(pdg[:, :], lhsT=KT(j), rhs=QT(q0 + 128 * m, q0 + 128 * m + 128), start=True, stop=True),
                     r=[ball], w=[bpdg])
                S.op("dve", lambda e: e.tensor_tensor(out=tdg[:], in0=pdg[:, :], in1=T, op=ALU.add), r=[bpdg, bT], w=[btdg])
                bd = bias_diag(j)
                S.op("act", lambda e: e.activation(out=pTd[:], in_=tdg[:], func=AF.Exp, bias=bd), r=[btdg, ball], w=[bpTd])
                pv_mm(pTd[:], 128 * m, 128 * m + 128, [bpTd], j == nj - 1)
        return ai

    def normalize(ai, Mv, den_row, out_ap, bout, extra_r=()):
        acc, bacc = pacc[ai], bpacc[ai]
        src, bsrc = (acc, bacc) if den_row > 0 else (pden[ai], bpden[ai])
        d0 = den_row
        S.op("dve", lambda e: e.reciprocal(out=rec[d0:d0 + 1, :], in_=src[d0:d0 + 1, :]), r=[bsrc], w=[brec])
        S.op("pe", lambda e: e.matmul(pbc[0:Mv, :], lhsT=ones_f[d0:d0 + 1, 0:Mv], rhs=rec[d0:d0 + 1, :], start=True, stop=True),
             r=[brec, bones], w=[bpbc])
        oi = nst[0] % 2
        S.op("act", lambda e: e.copy(out=osb[oi][0:Mv, :], in_=acc[0:Mv, :]), r=[bacc], w=[bosb[oi]])
        S.op("dve", lambda e: e.tensor_tensor(out=out_ap, in0=osb[oi][0:Mv, :], in1=pbc[0:Mv, :], op=ALU.mult),
             r=[bosb[oi], bpbc] + list(extra_r), w=[bout])

    for g in range(NCH):
        q0 = g * 512
        fi = g % 2
        for m in range(4):
            col = 4 * g + m
            S.op("dve", lambda e: e.tensor_copy(out=qaF[fi][0:32, m * 128:(m + 1) * 128],
                                                in_=dhi[0:32, col:col + 1].broadcast_to([32, 128])), r=[bcum], w=[bqaF[fi]])
            S.op("dve", lambda e: e.tensor_copy(out=qaF[fi][32:33, m * 128:(m + 1) * 128],
                                                in_=dlo[32:33, col:col + 1].broadcast_to([1, 128])), r=[bcum], w=[bqaF[fi]])
        ai = attn_group(g, lambda j: KTf[:, j * 128:(j + 1) * 128], lambda a, b: QTf[:, a:b],
                        lambda g_, j, c0: (kaugF[:, :], qaF[fi][:, c0:512], [bqaF[fi], bkaugF]),
                        lambda g_, j: tabF[:, g_, j:j + 1], lambda j: tabFd[:, j:j + 1], TF[:], bTF,
                        lambda j: Vf[:, j, :], 65, False)
        si = nst[0] % 2
        normalize(ai, 64, 64, ost[si][0:64, :], bost[si])
        toks.append(S.dma("pool", oT_d[0:64, q0:q0 + 512], ost[si][0:64, :], r=[bost[si]], slot=f"ost{si}"))
        nst[0] += 1
        ai = attn_group(g, lambda j: KTm[:, j * 128:(j + 1) * 128], lambda a, b: QTm[:, a:b], None,
                        lambda g_, j: 0.0, lambda j: 0.0, TM[:], bTM, lambda j: Vm[:, j, :], 65, False)
        si = nst[0] % 2
        normalize(ai, 64, 64, ost[si][0:64, :], bost[si])
        toks.append(S.dma("pool", oT_d[64:128, q0:q0 + 512], ost[si][0:64, :], r=[bost[si]], slot=f"ost{si}"))
        nst[0] += 1
        for mp in range(2):
            lo, hi = mp * 64, mp * 64 + 64
            ai = attn_group(g, lambda j: K12[lo:hi, j * 128:(j + 1) * 128], lambda a, b: Q12[lo:hi, a:b],
                            lambda g_, j, c0: (kaug[lo:hi, :], qaug[lo:hi, c0:512], [bqaug, bkaug]),
                            lambda g_, j: bsd[:, 4 * g_ - j + 3:4 * g_ - j + 4], lambda j: 0.0, TD[:], bTD,
                            lambda j: Vd[:, j, :], 128, True)
            if mp == 0:
                normalize(ai, 128, 0, o1n[:], bo1n)
            else:
                normalize(ai, 128, 0, od[:], bod)
            nst[0] += 1
        S.op("dve", lambda e: e.scalar_tensor_tensor(out=od[:], in0=od[:], scalar=nlam[:], in1=o1n[:], op0=ALU.mult, op1=ALU.add),
             r=[bod, bo1n, blam], w=[bod])
        S.op("act", lambda e: e.activation(out=sqd[:], in_=od[:], func=AF.Square), r=[bod], w=[bsqd])
        S.op("pe", lambda e: e.matmul(pbc[:, :], lhsT=ones_bf[:, :], rhs=sqd[:], start=True, stop=True), r=[bsqd, bones], w=[bpbc])
        S.op("act", lambda e: e.activation(out=rsd[:], in_=pbc[:, :], func=AF.Ln, bias=float(EPS), scale=1.0 / 128), r=[bpbc], w=[brsd])
        S.op("act", lambda e: e.activation(out=rsd[:], in_=rsd[:], func=AF.Exp, scale=-0.5), r=[brsd], w=[brsd])
        si = nst[0] % 2
        S.op("dve", lambda e: e.scalar_tensor_tensor(out=ost[si][:], in0=od[:], scalar=dgs[:], in1=rsd[:], op0=ALU.mult, op1=ALU.mult),
             r=[bod, brsd, bdgs], w=[bost[si]])
        toks.append(S.dma("pool", oT_d[128:256, q0:q0 + 512], ost[si][:], r=[bost[si]], slot=f"ost{si}"))
        nst[0] += 1
    S.finish(toks)
    S.barrier()
    C.release(mark0)
    return toks


def build_phaseA(SL, layer):
    nc = bass.Bass("TRN2", target_bir_lowering=False)
    dt = nc.dram_tensor
    I = "ExternalInput"
    xsT = dt("xsT", [8, 128, SL], BF16, kind=I).ap()
    w_sel = dt("w_sel", [D, NCOL], F32, kind=I).ap()
    ln1g = dt("ln1g", [128, 8], F32, kind=I).ap()
    wuq = dt("wuq", [256, 192], F32, kind=I).ap()
    qng = dt("qng", [128, 2], F32, kind=I).ap()
    wukv = dt("wukv", [128, 128], F32, kind=I).ap()
    kvng = dt("kvng", [128, 1], F32, kind=I).ap()
    dng = dt("dng", [128, 1], F32, kind=I).ap()
    fb = dt("fb", [128, 1], F32, kind=I).ap()
    lamv = dt("lamv", [256], F32, kind=I).ap()
    cosd = dt("cosd", [32, SL], F32, kind=I).ap()
    sind = dt("sind", [32, SL], F32, kind=I).ap()
    qaugd = dt("qaugd", [128, 512], BF16, kind=I).ap()
    kaugd = dt("kaugd", [128, 128], BF16, kind=I).ap()
    biasd = dt("biasd", [128, 64], F32, kind=I).ap()
    tdd = dt("tdd", [128, 128], F32, kind=I).ap()
    tfd = dt("tfd", [128, 128], F32, kind=I).ap()
    tmd = dt("tmd", [128, 128], F32, kind=I).ap()
    trid = dt("trid", [128, 128], F32, kind=I).ap()
    oT_d = dt("oT", [256, SL], BF16, kind="ExternalOutput").ap()
    C = Ctx(nc, "a")
    emit_phaseA(C, SL, layer, xsT, w_sel, ln1g, wuq, qng, wukv, kvng, dng, fb, lamv, cosd, sind, qaugd, kaugd, biasd,
                tdd, tfd, tmd, trid, oT_d)
    return nc


def host_consts(head, SL):
    bf = ml_dtypes.bfloat16
    slope = 2.0 ** (-8.0 * (head + 1) / 4)
    ql = np.arange(512)
    qa3 = np.stack([-slope * 128.0 * (ql // 128), -slope * (ql % 128), np.ones(512)]).astype(np.float32)
    kl = np.arange(128)
    ka3 = np.stack([np.ones(128), np.ones(128), slope * kl]).astype(np.float32)
    qaug = np.zeros((128, 512), np.float32)
    kaug = np.zeros((128, 128), np.float32)
    qaug[0:3] = qa3
    qaug[64:67] = qa3
    kaug[0:3] = ka3
    kaug[64:67] = ka3
    qaug = qaug.astype(bf)
    kaug = kaug.astype(bf)
    biasd = np.tile((-slope * 128.0 * (np.arange(64) - 3))[None, :], (128, 1)).astype(np.float32)
    k = np.arange(128)[:, None]
    q = np.arange(128)[None, :]
    vis = (k // 64) <= (q // 64)
    tdd = np.where(vis, -slope * np.abs(q - k), NEG).astype(np.float32)
    tfd = np.where(k <= q, 0.0, NEG).astype(np.float32)
    tmd = np.where(vis, 0.0, NEG).astype(np.float32)
    trid = (k <= q).astype(np.float32)
    inv = 1.0 / (10000.0 ** (np.arange(16, dtype=np.float32) / 16))
    ang = np.arange(SL, dtype=np.float32)[None, :] * inv[:, None]
    cos, sin = np.cos(ang).astype(np.float32), np.sin(ang).astype(np.float32)
    cosd = np.concatenate([cos, cos], 0)
    sind = np.concatenate([-sin, sin], 0)
    return dict(qaugd=qaug, kaugd=kaug, biasd=biasd, tdd=tdd, tfd=tfd, tmd=tmd, trid=trid,
                cosd=np.ascontiguousarray(cosd), sind=np.ascontiguousarray(sind))


def host_selA(inp, layer, head):
    w_in = inp["w_in"][layer]
    offs = np.cumsum([0, 256, 256, 256, 4, 512, 512, 512, 256, 128, 32])
    fq, fk, fv, ff, dq, dk, dv, cq, ckv, kr = [w_in[:, offs[i]:offs[i + 1]] for i in range(10)]
    h = head
    cols = [fq[:, h * 64:(h + 1) * 64], fk[:, h * 64:(h + 1) * 64], dq[:, h * 128:(h + 1) * 128], dk[:, h * 128:(h + 1) * 128],
            cq, ckv, kr, ckv[:, 0:64], kr[:, 16:32], kr[:, 0:16], fv[:, h * 64:(h + 1) * 64], ff[:, h:h + 1],
            dv[:, h * 128:(h + 1) * 128]]
    w_sel = np.ascontiguousarray(np.concatenate(cols, axis=1))
    assert w_sel.shape[1] == NCOL
    wuq_h = inp["w_uq"][layer][:, h * 96:(h + 1) * 96]
    wuq = np.ascontiguousarray(np.concatenate([wuq_h, wuq_h[:, 0:64], wuq_h[:, 80:96], wuq_h[:, 64:80]], axis=1))
    wukv = np.ascontiguousarray(inp["w_ukv"][layer][:, h * 128:(h + 1) * 128])
    lamv = np.concatenate([inp["lam_q1"][layer], inp["lam_k1"][layer], inp["lam_q2"][layer], inp["lam_k2"][layer]])
    return dict(w_sel=w_sel, ln1g=np.ascontiguousarray(inp["ln1_g"][layer].reshape(8, 128).T), wuq=wuq,
                qng=np.ascontiguousarray(inp["q_norm_g"][layer].reshape(2, 128).T), wukv=wukv,
                kvng=np.ascontiguousarray(inp["kv_norm_g"][layer].reshape(128, 1)),
                dng=np.ascontiguousarray(inp["diff_norm_g"][layer].reshape(128, 1)),
                fb=np.full((128, 1), inp["fgate_b"][layer][h], np.float32), lamv=np.ascontiguousarray(lamv))


def _wo_perm():
    idx = []
    for r in range(4):
        idx += list(range(r * 64, (r + 1) * 64))
        idx += list(range(768 + r * 64, 768 + (r + 1) * 64))
        idx += list(range(256 + r * 128, 256 + (r + 1) * 128))
    return np.array(idx)


def kernel(**inp):
    inp = {k: np.asarray(v) for k, v in inp.items()}
    bf = ml_dtypes.bfloat16
    x = np.ascontiguousarray(inp["x"], dtype=np.float32)
    cores = list(range(NCORE))
    consts = [host_consts(h, SEQ) for h in range(4)]
    xs = [np.ascontiguousarray(x[c // 4, (c % 4) * TOK:(c % 4 + 1) * TOK]) for c in cores]
    res = run_bass_kernel_spmd(build_phaseN(), [{"x": xs[c]} for c in cores], core_ids=cores).results
    xsT = [np.asarray(res[c]["xsT"]) for c in cores]
    perm = _wo_perm()
    out = None
    for layer in range(DEPTH):
        xsT_b = [np.ascontiguousarray(np.concatenate([xsT[b * 4 + j] for j in range(4)], axis=2)) for b in range(NB)]
        maps = []
        for c in cores:
            m = dict(xsT=xsT_b[c // 4])
            m.update(host_selA(inp, layer, c % 4))
            m.update(consts[c % 4])
            maps.append(m)
        res = run_bass_kernel_spmd(build_phaseA(SEQ, layer), maps, core_ids=cores).results
        oT_b = [np.concatenate([np.asarray(res[b * 4 + h]["oT"]) for h in range(4)], axis=0) for b in range(NB)]
        last = layer == DEPTH - 1
        w_o = np.ascontiguousarray(inp["w_o"][layer][perm])
        ln2 = np.ascontiguousarray(inp["ln2_g"][layer].reshape(8, 128).T)
        cw = np.ascontiguousarray(inp["conv_w"][layer].reshape(3, 44, 128).transpose(2, 0, 1))
        cb = np.ascontiguousarray(inp["conv_b"][layer].reshape(44, 128).T)
        maps = []
        for c in cores:
            b, j = c // 4, c % 4
            o = oT_b[b]
            oT = np.ascontiguousarray(o[:, j * TOK:(j + 1) * TOK].reshape(8, 128, TOK))
            if j == 0:
                oTh = np.zeros((8, 128, 128), bf)
                xh = np.zeros((128, D), np.float32)
            else:
                oTh = np.ascontiguousarray(o[:, j * TOK - 128:j * TOK].reshape(8, 128, 128))
                xh = np.ascontiguousarray(xs[c - 1][TOK - 128:])
            m = dict(x=xs[c], oT=oT, xh=xh, oTh=oTh, w_o=w_o, ln2_g=ln2, w_up=np.ascontiguousarray(inp["w_up"][layer]),
                     conv_w=cw, conv_b=cb, w_down=np.ascontiguousarray(inp["w_down"][layer]))
            if last:
                m["final_g"] = np.ascontiguousarray(inp["final_g"])
            maps.append(m)
        res = run_bass_kernel_spmd(build_phaseB(last), maps, core_ids=cores).results
        if last:
            out = np.stack([np.concatenate([np.asarray(res[b * 4 + j]["out"]) for j in range(4)], axis=0) for b in range(NB)])
        else:
            xs = [np.asarray(res[c]["xnew"]) for c in cores]
            xsT = [np.asarray(res[c]["xsT"]) for c in cores]
    return np.ascontiguousarray(out, dtype=np.float32)
```

```python
import numpy as np
import ml_dtypes
import concourse.bass as bass
import concourse.mybir as mybir
from concourse.bass_utils import run_bass_kernel_spmd

F32 = mybir.dt.float32
BF16 = mybir.dt.bfloat16
ALU = mybir.AluOpType
AF = mybir.ActivationFunctionType
AX = mybir.AxisListType

D = 1024
SEQ = 8192
NB = 2
DEPTH = 2
DFF = 2816
EPS = 1e-6
NEG = -30000.0
TOK = 2048
NCORE = 8


class Buf:
    __slots__ = ("w", "r", "name")

    def __init__(self, name=""):
        self.w = None
        self.r = []
        self.name = name


class Sched:
    def __init__(self, nc, tag=""):
        self.nc = nc
        self.E = {"pe": nc.tensor, "act": nc.scalar, "dve": nc.vector, "pool": nc.gpsimd, "sp": nc.sync}
        self.sem = {}
        self.cnt = {}
        self.seen = {k: {} for k in self.E}
        for k in self.E:
            self.sem[k] = nc.alloc_semaphore(f"s_{tag}{k}")
            self.cnt[k] = 0
        self.tag = tag
        self.ndma = 0
        self.dma_sems = {}

    def _wait(self, e, tok):
        if tok is None:
            return
        sem, val, key = tok
        if self.seen[e].get(key, 0) >= val:
            return
        self.E[e].wait_ge(sem, val)
        self.seen[e][key] = val

    def _deps(self, e, r, w):
        for b in r:
            self._wait(e, b.w)
        for b in w:
            self._wait(e, b.w)
            for t in b.r:
                self._wait(e, t)

    def _mark(self, tok, r, w):
        for b in r:
            b.r.append(tok)
        for b in w:
            b.w = tok
            b.r = []

    def op(self, e, fn, r=(), w=()):
        self._deps(e, r, w)
        fns = fn if isinstance(fn, (list, tuple)) else [fn]
        ins = None
        for f in fns:
            ins = f(self.E[e])
        self.cnt[e] += 1
        assert self.cnt[e] < 60000
        ins.then_inc(self.sem[e], 1)
        tok = (self.sem[e], self.cnt[e], e)
        self._mark(tok, r, w)
        return tok

    def dma(self, q, out, in_, r=(), w=(), slot=None):
        self._deps(q, r, w)
        if slot is None:
            slot = f"d{self.ndma % 8}"
        self.ndma += 1
        if slot not in self.dma_sems:
            self.dma_sems[slot] = [self.nc.alloc_semaphore(f"d_{self.tag}{slot}"), 0]
        ent = self.dma_sems[slot]
        if ent[1] > 0:
            self._wait(q, (ent[0], ent[1], "dma" + slot))
        ent[1] += 16
        assert ent[1] < 60000
        self.E[q].dma_start(out=out, in_=in_).then_inc(ent[0], 16)
        tok = (ent[0], ent[1], "dma" + slot)
        self._mark(tok, r, w)
        return tok

    def barrier(self):
        toks = [(self.sem[k], self.cnt[k], k) for k in self.E if self.cnt[k] > 0]
        toks += [(ent[0], ent[1], "dma" + slot) for slot, ent in self.dma_sems.items() if ent[1] > 0]
        for e in self.E:
            for t in toks:
                if t[2] != e:
                    self._wait(e, t)

    def finish(self, toks):
        for t in toks:
            self._wait("sp", t)


class Ctx:
    def __init__(self, nc, tag=""):
        self.nc = nc
        self.S = Sched(nc, tag)
        self.stack = []

    def sb(self, name, shape, dt):
        self.n = getattr(self, "n", 0) + 1
        g = self.nc.sbuf_tensor(f"{name}_{self.n}", list(shape), dt)
        t = g.__enter__()
        self.stack.append(g)
        return t

    def ps(self, name, shape, dt):
        self.n = getattr(self, "n", 0) + 1
        g = self.nc.psum_tensor(f"{name}_{self.n}", list(shape), dt)
        t = g.__enter__()
        self.stack.append(g)
        return t

    def release(self, mark=0):
        while len(self.stack) > mark:
            self.stack.pop().__exit__(None, None, None)


def emit_rstd(S, ss, r, nparts, n, bss, br, tmp=None):
    S.op("act", lambda e: e.activation(out=ss, in_=ss, func=AF.Sqrt, bias=float(EPS), scale=1.0 / n), r=[bss], w=[bss])
    S.op("dve", lambda e: e.reciprocal(out=r, in_=ss), r=[bss], w=[br])


def emit_tail(C, get_x, ntiles, xsT_dram=None, out_dram=None, g_bc=None, g_buf=None, ident=None, ident_buf=None):
    S, nc = C.S, C.nc
    mark = len(C.stack)
    junk = C.sb("t_junk", [128, D], BF16)
    bjunk = Buf()
    ss = [C.sb(f"t_ss{i}", [128, 1], F32) for i in range(2)]
    rr = [C.sb(f"t_r{i}", [128, 1], F32) for i in range(2)]
    bss = [Buf(), Buf()]
    brr = [Buf(), Buf()]
    toks = []
    if xsT_dram is not None:
        xs = [C.sb(f"t_xs{i}", [128, D], BF16) for i in range(2)]
        bxs = [Buf(), Buf()]
        pt = [C.ps(f"t_pt{i}", [128, 8, 128], BF16) for i in range(2)]
        bpt = [Buf(), Buf()]
        stage = [C.sb(f"t_st{i}", [128, 8, 512], BF16) for i in range(2)]
        bst = [Buf(), Buf()]
    else:
        xo = [C.sb(f"t_xo{i}", [128, D], F32) for i in range(2)]
        bxo = [Buf(), Buf()]
    for t in range(ntiles):
        xt, bx = get_x(t)
        i = t % 2
        S.op("act", lambda e: e.activation(out=junk[:], in_=xt, func=AF.Square, accum_out=ss[i][:]),
             r=[bx], w=[bjunk, bss[i]])
        emit_rstd(S, ss[i][:], rr[i][:], 128, D, bss[i], brr[i])
        if xsT_dram is not None:
            S.op("dve", lambda e: e.tensor_scalar(out=xs[i][:], in0=xt, scalar1=rr[i][:], scalar2=None, op0=ALU.mult),
                 r=[bx, brr[i]], w=[bxs[i]])
            S.op("pe", [(lambda e, c=c: e.transpose(out=pt[i][:, c, :], in_=xs[i][:, c * 128:(c + 1) * 128], identity=ident))
                        for c in range(8)], r=[bxs[i], ident_buf], w=[bpt[i]])
            g = (t // 4) % 2
            S.op("act", lambda e: e.copy(out=stage[g][:, :, (t % 4) * 128:(t % 4 + 1) * 128], in_=pt[i][:]),
                 r=[bpt[i]], w=[bst[g]])
            if t % 4 == 3:
                c0 = (t // 4) * 512
                toks.append(S.dma("pool", xsT_dram[:, :, c0:c0 + 512].rearrange("c p t -> p c t"), stage[g][:],
                                  r=[bst[g]], slot=f"tst{g}"))
        else:
            S.op("dve", lambda e: e.scalar_tensor_tensor(out=xo[i][:], in0=xt, scalar=rr[i][:], in1=g_bc,
                                                          op0=ALU.mult, op1=ALU.mult),
                 r=[bx, brr[i], g_buf], w=[bxo[i]])
            toks.append(S.dma("pool", out_dram[t * 128:(t + 1) * 128, :], xo[i][:], r=[bxo[i]], slot=f"tout{i}"))
    return toks, mark


def load_ident(C):
    S, nc = C.S, C.nc
    ident = C.sb("ident", [128, 128], BF16)
    b = Buf()
    ones = C.sb("ident_ones", [128, 128], BF16)
    bo = Buf()
    S.op("pool", lambda e: e.memset(ones[:], 1.0), w=[bo])
    S.op("pool", lambda e: e.affine_select(out=ident[:], in_=ones[:], pattern=[[-1, 128]], compare_op=ALU.is_equal,
                                           fill=0.0, base=0, channel_multiplier=1), r=[bo], w=[b])
    return ident, b


def build_phaseN():
    nc = bass.Bass("TRN2", target_bir_lowering=False)
    x = nc.dram_tensor("x", [TOK, D], F32, kind="ExternalInput").ap()
    xsT = nc.dram_tensor("xsT", [8, 128, TOK], BF16, kind="ExternalOutput").ap()
    C = Ctx(nc, "nn")
    S = C.S
    ident, bid = load_ident(C)
    xb = [C.sb(f"xb{i}", [128, D], F32) for i in range(2)]
    bxb = [Buf(), Buf()]

    def get_x(t):
        i = t % 2
        S.dma("sp", xb[i][:], x[t * 128:(t + 1) * 128, :], w=[bxb[i]], slot=f"xl{i}")
        return xb[i][:], bxb[i]

    toks, _ = emit_tail(C, get_x, TOK // 128, xsT_dram=xsT, ident=ident[:], ident_buf=bid)
    S.finish(toks)
    C.release()
    return nc


def prep_weight(C, wsrc, dst, ncols, k, stg, bstg, bdst, scale=None, bscale=None):
    S = C.S
    i = k % 2
    S.dma("sp", stg[i][:, :ncols], wsrc, w=[bstg[i]], slot=f"wst{i}")
    eng = "dve" if k % 2 == 0 else "pool"
    if scale is None:
        S.op(eng, lambda e: e.tensor_copy(out=dst, in_=stg[i][:, :ncols]), r=[bstg[i]], w=[bdst])
    else:
        S.op(eng, lambda e: e.tensor_scalar(out=dst, in0=stg[i][:, :ncols], scalar1=scale, scalar2=None, op0=ALU.mult),
             r=[bstg[i], bscale], w=[bdst])


def emit_phaseB(C, x_in, oT_in, xh_in, oTh_in, w_o, ln2_g, w_up, conv_w, conv_b, w_down, xmid_d, xs2T_d, xnew_d):
    S, nc = C.S, C.nc
    mark0 = len(C.stack)
    ident, bid = load_ident(C)
    stg = [C.sb(f"wstg{i}", [128, 1408], F32) for i in range(2)]
    bstg = [Buf(), Buf()]
    mark1 = len(C.stack)
    wo = C.sb("wo", [128, 8, D], BF16)
    bwo = Buf()
    for kc in range(8):
        prep_weight(C, w_o[kc * 128:(kc + 1) * 128, :], wo[:, kc, :], D, kc, stg, bstg, bwo)
    xb = [C.sb(f"b1x{i}", [128, D], F32) for i in range(2)]
    bxb = [Buf(), Buf()]
    ob = [C.sb(f"b1o{i}", [128, 8, 128], BF16) for i in range(2)]
    bob = [Buf(), Buf()]
    pm = [C.ps(f"b1pm{i}", [128, 2, 512], F32) for i in range(2)]
    bpm = [Buf(), Buf()]
    xm = [C.sb(f"b1xm{i}", [128, D], F32) for i in range(2)]
    bxm = [Buf(), Buf()]
    junk = C.sb("b1junk", [128, D], BF16)
    bjunk = Buf()
    ss = [C.sb(f"b1ss{i}", [128, 1], F32) for i in range(2)]
    rr = [C.sb(f"b1r{i}", [128, 1], F32) for i in range(2)]
    bss = [Buf(), Buf()]
    brr = [Buf(), Buf()]
    xs = [C.sb(f"b1xs{i}", [128, D], BF16) for i in range(2)]
    bxs = [Buf(), Buf()]
    pt = [C.ps(f"b1pt{i}", [128, 8, 128], BF16) for i in range(2)]
    bpt = [Buf(), Buf()]
    st = [C.sb(f"b1st{i}", [128, 8, 128], BF16) for i in range(2)]
    bst = [Buf(), Buf()]
    bxmid_d = Buf()
    bxs2T_d = Buf()
    NT = TOK // 128
    for t in range(-1, NT):
        i = t % 2
        if t < 0:
            S.dma("sp", xb[i][:], xh_in[:, :], w=[bxb[i]], slot=f"b1x{i}")
            S.dma("sp", ob[i][:], oTh_in.rearrange("c p t -> p c t"), w=[bob[i]], slot=f"b1o{i}")
        else:
            S.dma("sp", xb[i][:], x_in[t * 128:(t + 1) * 128, :], w=[bxb[i]], slot=f"b1x{i}")
            S.dma("sp", ob[i][:], oT_in[:, :, t * 128:(t + 1) * 128].rearrange("c p t -> p c t"), w=[bob[i]],
                  slot=f"b1o{i}")
        for h in range(2):
            S.op("pe", [(lambda e, kc=kc, h=h: e.matmul(pm[i][:, h, :], lhsT=ob[i][:, kc, :],
                                                        rhs=wo[:, kc, h * 512:(h + 1) * 512],
                                                        start=(kc == 0), stop=(kc == 7))) for kc in range(8)],
                 r=[bob[i], bwo], w=[bpm[i]])
        S.op("dve", lambda e: e.tensor_tensor(out=xm[i][:], in0=pm[i][:].rearrange("p a b -> p (a b)"), in1=xb[i][:],
                                              op=ALU.add), r=[bpm[i], bxb[i]], w=[bxm[i]])
        if t >= 0:
            S.dma("pool", xmid_d[t * 128:(t + 1) * 128, :], xm[i][:], r=[bxm[i]], w=[bxmid_d], slot=f"b1xm{i}")
        S.op("act", lambda e: e.activation(out=junk[:], in_=xm[i][:], func=AF.Square, accum_out=ss[i][:]),
             r=[bxm[i]], w=[bjunk, bss[i]])
        emit_rstd(S, ss[i][:], rr[i][:], 128, D, bss[i], brr[i])
        S.op("dve", lambda e: e.tensor_scalar(out=xs[i][:], in0=xm[i][:], scalar1=rr[i][:], scalar2=None, op0=ALU.mult),
             r=[bxm[i], brr[i]], w=[bxs[i]])
        S.op("pe", [(lambda e, c=c: e.transpose(out=pt[i][:, c, :], in_=xs[i][:, c * 128:(c + 1) * 128], identity=ident[:]))
                    for c in range(8)], r=[bxs[i], bid], w=[bpt[i]])
        S.op("act", lambda e: e.copy(out=st[i][:], in_=pt[i][:]), r=[bpt[i]], w=[bst[i]])
        c0 = (t + 1) * 128
        S.dma("pool", xs2T_d[:, :, c0:c0 + 128].rearrange("c p t -> p c t"), st[i][:], r=[bst[i]], w=[bxs2T_d],
              slot=f"b1st{i}")
    C.release(mark1)
    S.barrier()
    g2 = C.sb("ln2g", [128, 8], F32)
    bg2 = Buf()
    S.dma("sp", g2[:], ln2_g[:, :], w=[bg2], slot="cst")
    cw = C.sb("convw", [128, 3, 44], F32)
    cb = C.sb("convb", [128, 44], F32)
    bcw = Buf()
    S.dma("sp", cw[:], conv_w[:, :, :], w=[bcw], slot="cst")
    S.dma("sp", cb[:], conv_b[:, :], w=[bcw], slot="cst")
    wup = C.sb("wup", [128, 8, 2 * DFF], BF16)
    bwup = Buf()
    k = 0
    for kc in range(8):
        for h in range(4):
            prep_weight(C, w_up[kc * 128:(kc + 1) * 128, h * 1408:(h + 1) * 1408], wup[:, kc, h * 1408:(h + 1) * 1408], 1408,
                        k, stg, bstg, bwup, scale=g2[:, kc:kc + 1], bscale=bg2)
            k += 1
    wdn = C.sb("wdn", [128, 22, D], BF16)
    bwdn = Buf()
    for fc in range(0, 22, 2):
        for a in range(2):
            prep_weight(C, w_down[(fc + a) * 128:(fc + a + 1) * 128, :], wdn[:, fc + a, :], D, k, stg, bstg, bwdn)
            k += 1
    xg = [C.sb(f"b2xg{i}", [128, 8, 512], BF16) for i in range(2)]
    xh = [C.sb(f"b2xh{i}", [128, 8, 2], BF16) for i in range(2)]
    bxg = [Buf(), Buf()]
    pg = [C.ps(f"b2pg{i}", [128, 512], F32) for i in range(2)]
    pu = [C.ps(f"b2pu{i}", [128, 512], F32) for i in range(2)]
    ph = [C.ps(f"b2ph{i}", [128, 4], F32) for i in range(2)]
    bpg = [Buf(), Buf()]
    bpu = [Buf(), Buf()]
    bph = [Buf(), Buf()]
    yg = [C.sb(f"b2yg{i}", [128, 512], F32) for i in range(2)]
    yu = [C.sb(f"b2yu{i}", [128, 512], F32) for i in range(2)]
    hs = [C.sb(f"b2hs{i}", [128, 4], F32) for i in range(2)]
    byg = [Buf(), Buf()]
    byu = [Buf(), Buf()]
    bhs = [Buf(), Buf()]
    sg = [C.sb(f"b2sg{i}", [128, 512], F32) for i in range(2)]
    bsg = [Buf(), Buf()]
    hT = C.sb("b2hT", [128, 22, 512], BF16)
    bhT = [Buf() for _ in range(22)]
    pd = C.ps("b2pd", [128, 2, 512], F32)
    bpd = Buf()
    xmb = [C.sb(f"b2xm{i}", [128, D], F32) for i in range(2)]
    bxmb = [Buf(), Buf()]
    xnb = xmb
    bxnb = bxmb
    bxnew_d = Buf()
    cnt = 0
    for gi in range(TOK // 512):
        i = gi % 2
        c0 = 128 + gi * 512
        S.dma("sp", xg[i][:], xs2T_d[:, :, c0:c0 + 512].rearrange("c p t -> p c t"), r=[bxs2T_d], w=[bxg[i]],
              slot=f"b2xg{i}")
        S.dma("sp", xh[i][:], xs2T_d[:, :, c0 - 2:c0].rearrange("c p t -> p c t"), r=[bxs2T_d], w=[bxg[i]],
              slot=f"b2xh{i}")
        for fc in range(22):
            j = cnt % 2
            cnt += 1
            for (pp, bpp, col0) in ((pg, bpg, fc * 128), (pu, bpu, DFF + fc * 128)):
                S.op("pe", [(lambda e, kc=kc, pp=pp, col0=col0: e.matmul(pp[j][:], lhsT=wup[:, kc, col0:col0 + 128],
                                                                        rhs=xg[i][:, kc, :], start=(kc == 0),
                                                                        stop=(kc == 7))) for kc in range(8)],
                     r=[bwup, bxg[i]], w=[bpp[j]])
            for hh, col0 in ((0, fc * 128), (1, DFF + fc * 128)):
                S.op("pe", [(lambda e, kc=kc, hh=hh, col0=col0: e.matmul(ph[j][:, hh * 2:hh * 2 + 2],
                                                                        lhsT=wup[:, kc, col0:col0 + 128],
                                                                        rhs=xh[i][:, kc, :], start=(kc == 0),
                                                                        stop=(kc == 7))) for kc in range(8)],
                     r=[bwup, bxg[i]], w=[bph[j]])
            S.op("act", lambda e: e.copy(out=hs[j][:], in_=ph[j][:]), r=[bph[j]], w=[bhs[j]])
            for (pp, bpp, yy, byy, ch, hoff) in ((pg, bpg, yg, byg, fc, 0), (pu, bpu, yu, byu, 22 + fc, 2)):
                w0 = cw[:, 0, ch:ch + 1]
                w1 = cw[:, 1, ch:ch + 1]
                w2 = cw[:, 2, ch:ch + 1]
                S.op("act", lambda e: e.activation(out=yy[j][:], in_=pp[j][:], func=AF.Identity, bias=cb[:, ch:ch + 1],
                                                   scale=w2), r=[bpp[j], bcw], w=[byy[j]])
                S.op("dve", lambda e: e.scalar_tensor_tensor(out=yy[j][:, 1:512], in0=pp[j][:, 0:511], scalar=w1,
                                                              in1=yy[j][:, 1:512], op0=ALU.mult, op1=ALU.add),
                     r=[bpp[j], bcw, byy[j]], w=[byy[j]])
                S.op("dve", lambda e: e.scalar_tensor_tensor(out=yy[j][:, 2:512], in0=pp[j][:, 0:510], scalar=w0,
                                                              in1=yy[j][:, 2:512], op0=ALU.mult, op1=ALU.add),
                     r=[bpp[j], bcw, byy[j]], w=[byy[j]])
                S.op("dve", lambda e: e.scalar_tensor_tensor(out=yy[j][:, 0:2], in0=hs[j][:, hoff:hoff + 2], scalar=w0,
                                                              in1=yy[j][:, 0:2], op0=ALU.mult, op1=ALU.add),
                     r=[bhs[j], bcw, byy[j]], w=[byy[j]])
                S.op("dve", lambda e: e.scalar_tensor_tensor(out=yy[j][:, 0:1], in0=hs[j][:, hoff + 1:hoff + 2], scalar=w1,
                                                              in1=yy[j][:, 0:1], op0=ALU.mult, op1=ALU.add),
                     r=[bhs[j], bcw, byy[j]], w=[byy[j]])
            S.op("act", lambda e: e.activation(out=sg[j][:], in_=yg[j][:], func=AF.Silu), r=[byg[j]], w=[bsg[j]])
            S.op("pool", lambda e: e.tensor_tensor(out=hT[:, fc, :], in0=sg[j][:], in1=yu[j][:], op=ALU.mult),
                 r=[bsg[j], byu[j]], w=[bhT[fc]])
        for tt in range(4):
            t = gi * 4 + tt
            m = t % 2
            S.dma("sp", xmb[m][:], xmid_d[t * 128:(t + 1) * 128, :], r=[bxmid_d], w=[bxmb[m]], slot=f"b2xm{m}")
            for h in range(2):
                S.op("pe", [(lambda e, fc=fc, h=h: e.matmul(pd[:, h, :], lhsT=hT[:, fc, tt * 128:(tt + 1) * 128],
                                                            rhs=wdn[:, fc, h * 512:(h + 1) * 512], start=(fc == 0),
                                                            stop=(fc == 21))) for fc in range(22)],
                     r=bhT + [bwdn], w=[bpd])
            S.op("dve", lambda e: e.tensor_tensor(out=xnb[m][:], in0=pd[:].rearrange("p a b -> p (a b)"), in1=xmb[m][:],
                                                  op=ALU.add), r=[bpd], w=[bxnb[m]])
            S.dma("pool", xnew_d[t * 128:(t + 1) * 128, :], xnb[m][:], r=[bxnb[m]], w=[bxnew_d], slot=f"b2xn{m}")
    C.release(mark0)
    S.barrier()
    return bxnew_d


def build_phaseB(last):
    nc = bass.Bass("TRN2", target_bir_lowering=False)
    dt = nc.dram_tensor
    x_in = dt("x", [TOK, D], F32, kind="ExternalInput").ap()
    oT_in = dt("oT", [8, 128, TOK], BF16, kind="ExternalInput").ap()
    xh_in = dt("xh", [128, D], F32, kind="ExternalInput").ap()
    oTh_in = dt("oTh", [8, 128, 128], BF16, kind="ExternalInput").ap()
    w_o = dt("w_o", [D, D], F32, kind="ExternalInput").ap()
    ln2_g = dt("ln2_g", [128, 8], F32, kind="ExternalInput").ap()
    w_up = dt("w_up", [D, 2 * DFF], F32, kind="ExternalInput").ap()
    conv_w = dt("conv_w", [128, 3, 44], F32, kind="ExternalInput").ap()
    conv_b = dt("conv_b", [128, 44], F32, kind="ExternalInput").ap()
    w_down = dt("w_down", [DFF, D], F32, kind="ExternalInput").ap()
    xmid_d = dt("xmid_s", [TOK, D], F32, kind="Internal").ap()
    xs2T_d = dt("xs2T_s", [8, 128, 128 + TOK], BF16, kind="Internal").ap()
    C = Ctx(nc, "b")
    S = C.S
    if last:
        fg = dt("final_g", [D], F32, kind="ExternalInput").ap()
        xnew_d = dt("xnew_s", [TOK, D], F32, kind="Internal").ap()
        out_d = dt("out", [TOK, D], F32, kind="ExternalOutput").ap()
    else:
        xnew_d = dt("xnew", [TOK, D], F32, kind="ExternalOutput").ap()
        xsT_d = dt("xsT", [8, 128, TOK], BF16, kind="ExternalOutput").ap()
    bxn = emit_phaseB(C, x_in, oT_in, xh_in, oTh_in, w_o, ln2_g, w_up, conv_w, conv_b, w_down, xmid_d, xs2T_d, xnew_d)
    ident, bid = load_ident(C)
    xb = [C.sb(f"xb{i}", [128, D], F32) for i in range(2)]
    bxb = [Buf(), Buf()]

    def get_x(t):
        i = t % 2
        S.dma("sp", xb[i][:], xnew_d[t * 128:(t + 1) * 128, :], r=[bxn], w=[bxb[i]], slot=f"xl{i}")
        return xb[i][:], bxb[i]

    if last:
        gbc = C.sb("gbc", [128, D], F32)
        bg = Buf()
        S.dma("sp", gbc[:], fg.partition_broadcast(128), w=[bg], slot="cst2")
        toks, _ = emit_tail(C, get_x, TOK // 128, out_dram=out_d, g_bc=gbc[:], g_buf=bg)
    else:
        toks, _ = emit_tail(C, get_x, TOK // 128, xsT_dram=xsT_d, ident=ident[:], ident_buf=bid)
    S.barrier()
    C.release()
    return nc


FQ, FK, DQ, DK, CQ, CKV, KR, FILL, KRS, FV, FF, DV = 0, 64, 128, 256, 384, 640, 768, 800, 864, 896, 960, 961
NCOL = 1089


def emit_phaseA(C, SL, layer, xsT, w_sel, ln1g, wuq, qng, wukv, kvng, dng, fb, lamv, cosd, sind,
                qaugd, kaugd, biasd, tdd, tfd, tmd, trid, oT_d):
    S, nc = C.S, C.nc
    NCH = SL // 512
    NBLK = SL // 128
    lam_init = 0.8 - 0.6 * float(np.exp(-0.3 * layer))
    mark0 = len(C.stack)
    cst = {}

    def cload(name, src, shape, dt, q="sp"):
        t = C.sb(name, shape, dt)
        b = Buf()
        S.dma(q, t[:], src, w=[b], slot="cA")
        cst[name] = (t, b)
        return t, b

    g1, bg1 = cload("ln1g", ln1g[:, :], [128, 8], F32)
    qg, bqg = cload("qng", qng[:, :], [128, 2], F32)
    kvg, bkvg = cload("kvng", kvng[:, :], [128, 1], F32)
    dg, bdg = cload("dng", dng[:, :], [128, 1], F32)
    fbt, bfb = cload("fb", fb[:, :], [128, 1], F32)
    lmv, blmv = cload("lamv", lamv.partition_broadcast(128), [128, 256], F32)
    qaug, bqaug = cload("qaugd", qaugd[:, :], [128, 512], BF16)
    kaug, bkaug = cload("kaugd", kaugd[:, :], [128, 128], BF16)
    bsd, bbsd = cload("biasd", biasd[:, :], [128, 64], F32)
    TD, bTD = cload("tdd", tdd[:, :], [128, 128], F32)
    TF, bTF = cload("tfd", tfd[:, :], [128, 128], F32)
    TM, bTM = cload("tmd", tmd[:, :], [128, 128], F32)
    tri, btri = cload("trid", trid[:, :], [128, 128], F32)
    ones_bf = C.sb("ones_bf", [128, 128], BF16)
    ones_f = C.sb("ones_f", [128, 128], F32)
    zer_bf = C.sb("zer_bf", [128, 128], BF16)
    bones = Buf()
    S.op("pool", lambda e: e.memset(ones_bf[:], 1.0), w=[bones])
    S.op("pool", lambda e: e.memset(ones_f[:], 1.0), w=[bones])
    S.op("pool", lambda e: e.memset(zer_bf[:], 0.0), w=[bones])
    kaugF = C.sb("kaugF", [64, 128], BF16)
    bkaugF = Buf()
    S.op("pool", lambda e: e.memset(kaugF[:], 0.0), w=[bkaugF])
    S.op("pool", lambda e: e.memset(kaugF[0:1, :], 1.0), w=[bkaugF])
    S.op("pool", lambda e: e.memset(kaugF[32:33, :], 1.0), w=[bkaugF])
    nfb = C.sb("nfb", [128, 1], F32)
    bnfb = Buf()
    S.op("dve", lambda e: e.tensor_scalar(out=nfb[:], in0=fbt[:], scalar1=-1.0, scalar2=None, op0=ALU.mult), r=[bfb], w=[bnfb])
    dgs = C.sb("dgs", [128, 1], F32)
    bdgs = Buf()
    S.op("dve", lambda e: e.tensor_scalar(out=dgs[:], in0=dg[:], scalar1=float(1.0 - lam_init), scalar2=None, op0=ALU.mult),
         r=[bdg], w=[bdgs])
    lprod = C.sb("lprod", [128, 2, 64], F32)
    ldot = C.sb("ldot", [128, 2], F32)
    nlam = C.sb("nlam", [128, 1], F32)
    blam = Buf()
    lm3 = lmv[:].rearrange("p (a d) -> p a d", a=4)
    S.op("dve", lambda e: e.tensor_tensor(out=lprod[:, 0, :], in0=lm3[:, 0, :], in1=lm3[:, 1, :], op=ALU.mult), r=[blmv], w=[blam])
    S.op("dve", lambda e: e.tensor_tensor(out=lprod[:, 1, :], in0=lm3[:, 2, :], in1=lm3[:, 3, :], op=ALU.mult), r=[blmv, blam], w=[blam])
    S.op("dve", lambda e: e.tensor_reduce(out=ldot[:], in_=lprod[:], axis=AX.X, op=ALU.add), r=[blam], w=[blam])
    S.op("act", lambda e: e.activation(out=ldot[:], in_=ldot[:], func=AF.Exp), r=[blam], w=[blam])
    S.op("dve", lambda e: e.tensor_tensor(out=nlam[:], in0=ldot[:, 1:2], in1=ldot[:, 0:1], op=ALU.subtract), r=[blam], w=[blam])
    S.op("dve", lambda e: e.tensor_scalar(out=nlam[:], in0=nlam[:], scalar1=float(-lam_init), scalar2=None, op0=ALU.add),
         r=[blam], w=[blam])
    stg = [C.sb(f"astg{i}", [128, NCOL], F32) for i in range(2)]
    bstg = [Buf(), Buf()]
    wsel = C.sb("wsel", [128, 8, NCOL], BF16)
    bwsel = Buf()
    for kc in range(8):
        prep_weight(C, w_sel[kc * 128:(kc + 1) * 128, :], wsel[:, kc, :], NCOL, kc, stg, bstg, bwsel,
                    scale=g1[:, kc:kc + 1], bscale=bg1)
    wq = C.sb("wuq", [128, 2, 192], BF16)
    bwq = Buf()
    qgs = C.sb("qgs", [128, 2], F32)
    bqgs = Buf()
    S.op("dve", lambda e: e.tensor_scalar(out=qgs[:], in0=qg[:], scalar1=float(96 ** -0.5), scalar2=None, op0=ALU.mult),
         r=[bqg], w=[bqgs])
    for a in range(2):
        prep_weight(C, wuq[a * 128:(a + 1) * 128, :], wq[:, a, :], 192, a, stg, bstg, bwq, scale=qgs[:, a:a + 1], bscale=bqgs)
    wkv = C.sb("wukv", [128, 128], BF16)
    bwkv = Buf()
    prep_weight(C, wukv[:, :], wkv[:], 128, 0, stg, bstg, bwkv, scale=kvg[:, 0:1], bscale=bkvg)
    QTf = C.sb("QTf", [64, SL], BF16)
    KTf = C.sb("KTf", [64, SL], BF16)
    Vf = C.sb("Vf", [128, NBLK, 65], BF16)
    Q12 = C.sb("Q12", [128, SL], BF16)
    K12 = C.sb("K12", [128, SL], BF16)
    Vd = C.sb("Vd", [128, NBLK, 128], BF16)
    QTm = C.sb("QTm", [96, SL], BF16)
    KTm = C.sb("KTm", [96, SL], BF16)
    Vm = C.sb("Vm", [128, NBLK, 65], BF16)
    Fcol = C.sb("Fcol", [128, NBLK], F32)
    bqkv = [Buf() for _ in range(NCH)]
    bvone = Buf()
    S.op("pool", lambda e: e.memset(Vf[:, :, 64:65], 1.0), w=[bvone])
    S.op("pool", lambda e: e.memset(Vm[:, :, 64:65], 1.0), w=[bvone])
    mark1 = len(C.stack)
    xc = [C.sb(f"xc{i}", [128, 8, 512], BF16) for i in range(2)]
    bxc = [Buf(), Buf()]
    pf = [C.ps(f"pf{i}", [128, 512], F32) for i in range(3)]
    bpf = [Buf() for _ in range(3)]
    pv = [C.ps(f"pv{i}", [128, 2, 256], F32) for i in range(2)]
    bpv = [Buf(), Buf()]
    psm = C.ps("psm", [128, 8], F32)
    bpsm = Buf()
    pvm = C.ps("pvm", [128, 4, 64], F32)
    bpvm = Buf()
    cqT = C.sb("cqT", [128, 2, 512], BF16)
    sqq = C.sb("sqq", [128, 2, 512], BF16)
    bcq = Buf()
    ckvT = C.sb("ckvT", [128, 512], BF16)
    sqkv = C.sb("sqkv", [128, 512], BF16)
    bckv = Buf()
    rq = C.sb("rq", [96, 512], F32)
    brq = Buf()
    rkv = C.sb("rkv", [64, 512], F32)
    brkv = Buf()
    rkc = C.sb("rkc", [128, 4], F32)
    brkc = Buf()
    cosT = [C.sb(f"cosT{i}", [96, 512], F32) for i in range(2)]
    sinT = [C.sb(f"sinT{i}", [96, 512], F32) for i in range(2)]
    bcs = [Buf(), Buf()]
    t1 = C.sb("t1", [96, 512], F32)
    t2 = C.sb("t2", [96, 512], F32)
    bt1 = Buf()
    bt2 = Buf()
    npf = [0]

    def fgroup(cols, M, xi, kcs=8):
        i = npf[0] % 3
        npf[0] += 1
        S.op("pe", [(lambda e, kc=kc: e.matmul(pf[i][0:M, :], lhsT=wsel[:, kc, cols:cols + M], rhs=xc[xi][:, kc, :],
                                               start=(kc == 0), stop=(kc == 7))) for kc in range(8)],
             r=[bwsel, bxc[xi]], w=[bpf[i]])
        return pf[i], bpf[i]

    def rstd_from_ps(p, bp, M, n, out, bout):
        S.op("act", lambda e: e.activation(out=out, in_=p, func=AF.Ln, bias=float(EPS), scale=1.0 / n), r=[bp], w=[bout])
        S.op("act", lambda e: e.activation(out=out, in_=out, func=AF.Exp, scale=-0.5), r=[bout], w=[bout])

    for c in range(NCH):
        xi = c % 2
        t0 = c * 512
        bo = bqkv[c]
        S.dma("sp", xc[xi][:], xsT[:, :, t0:t0 + 512].rearrange("c p t -> p c t"), w=[bxc[xi]], slot=f"xc{xi}")
        S.dma("sp", cosT[xi][64:96, :], cosd[:, t0:t0 + 512], w=[bcs[xi]], slot=f"cs{xi}")
        S.dma("sp", sinT[xi][64:96, :], sind[:, t0:t0 + 512], w=[bcs[xi]], slot=f"cs{xi}")
        p, bp = fgroup(FQ, 64, xi)
        S.op("act", lambda e: e.activation(out=QTf[:, t0:t0 + 512], in_=p[0:64, :], func=AF.Copy, scale=0.125), r=[bp], w=[bo])
        p, bp = fgroup(FK, 64, xi)
        S.op("dve", lambda e: e.tensor_copy(out=KTf[:, t0:t0 + 512], in_=p[0:64, :]), r=[bp], w=[bo])
        p, bp = fgroup(DQ, 128, xi)
        S.op("act", lambda e: e.activation(out=Q12[:, t0:t0 + 512], in_=p[:, :], func=AF.Copy, scale=0.125), r=[bp], w=[bo])
        p, bp = fgroup(DK, 128, xi)
        S.op("dve", lambda e: e.tensor_copy(out=K12[:, t0:t0 + 512], in_=p[:, :]), r=[bp], w=[bo])
        for a in range(2):
            p, bp = fgroup(CQ + a * 128, 128, xi)
            S.op("dve", lambda e: e.tensor_copy(out=cqT[:, a, :], in_=p[:, :]), r=[bp], w=[bcq])
            S.op("act", lambda e: e.activation(out=sqq[:, a, :], in_=p[:, :], func=AF.Square), r=[bp], w=[bcq])
        i = npf[0] % 3
        npf[0] += 1
        S.op("pe", [(lambda e, a=a: e.matmul(pf[i][0:96, :], lhsT=ones_bf[:, 0:96], rhs=sqq[:, a, :], start=(a == 0),
                                             stop=(a == 1))) for a in range(2)], r=[bones, bcq], w=[bpf[i]])
        rstd_from_ps(pf[i][0:96, :], bpf[i], 96, 256, rq[:], brq)
        i1 = npf[0] % 3
        npf[0] += 1
        S.op("pe", [(lambda e, a=a: e.matmul(pf[i1][0:96, :], lhsT=wq[:, a, 0:96], rhs=cqT[:, a, :], start=(a == 0),
                                             stop=(a == 1))) for a in range(2)], r=[bwq, bcq], w=[bpf[i1]])
        i2 = npf[0] % 3
        npf[0] += 1
        S.op("pe", [(lambda e, a=a: e.matmul(pf[i2][0:96, :], lhsT=wq[:, a, 96:192], rhs=cqT[:, a, :], start=(a == 0),
                                             stop=(a == 1))) for a in range(2)], r=[bwq, bcq], w=[bpf[i2]])
        S.op("dve", lambda e: e.tensor_tensor(out=QTm[0:64, t0:t0 + 512], in0=pf[i1][0:64, :], in1=rq[0:64, :], op=ALU.mult),
             r=[bpf[i1], brq], w=[bo])
        S.op("dve", lambda e: e.tensor_tensor(out=t1[64:96, :], in0=pf[i1][64:96, :], in1=cosT[xi][64:96, :], op=ALU.mult),
             r=[bpf[i1], bcs[xi]], w=[bt1])
        S.op("dve", lambda e: e.tensor_tensor(out=t2[64:96, :], in0=pf[i2][64:96, :], in1=sinT[xi][64:96, :], op=ALU.mult),
             r=[bpf[i2], bcs[xi]], w=[bt2])
        S.op("pool", lambda e: e.tensor_tensor(out=t1[64:96, :], in0=t1[64:96, :], in1=t2[64:96, :], op=ALU.add),
             r=[bt1, bt2], w=[bt1])
        S.op("pool", lambda e: e.tensor_tensor(out=QTm[64:96, t0:t0 + 512], in0=t1[64:96, :], in1=rq[64:96, :], op=ALU.mult),
             r=[bt1, brq], w=[bo])
        p, bp = fgroup(CKV, 128, xi)
        S.op("dve", lambda e: e.tensor_copy(out=ckvT[:], in_=p[:, :]), r=[bp], w=[bckv])
        S.op("act", lambda e: e.activation(out=sqkv[:], in_=p[:, :], func=AF.Square), r=[bp], w=[bckv])
        i = npf[0] % 3
        npf[0] += 1
        S.op("pe", lambda e: e.matmul(pf[i][0:64, :], lhsT=ones_bf[:, 0:64], rhs=sqkv[:], start=True, stop=True),
             r=[bones, bckv], w=[bpf[i]])
        rstd_from_ps(pf[i][0:64, :], bpf[i], 64, 128, rkv[:], brkv)
        i = npf[0] % 3
        npf[0] += 1
        S.op("pe", lambda e: e.matmul(pf[i][0:64, :], lhsT=wkv[:, 0:64], rhs=ckvT[:], start=True, stop=True),
             r=[bwkv, bckv], w=[bpf[i]])
        S.op("dve", lambda e: e.tensor_tensor(out=KTm[0:64, t0:t0 + 512], in0=pf[i][0:64, :], in1=rkv[:], op=ALU.mult),
             r=[bpf[i], brkv], w=[bo])
        S.op("pe", [(lambda e, tb=tb: e.matmul(psm[:, tb:tb + 1], lhsT=sqkv[:, tb * 128:(tb + 1) * 128], rhs=ones_bf[:, 0:1],
                                               start=True, stop=True)) for tb in range(4)], r=[bones, bckv], w=[bpsm])
        rstd_from_ps(psm[:, 0:4], bpsm, 128, 128, rkc[:], brkc)
        S.op("pe", [(lambda e, tb=tb: e.matmul(pvm[:, tb, :], lhsT=ckvT[:, tb * 128:(tb + 1) * 128], rhs=wkv[:, 64:128],
                                               start=True, stop=True)) for tb in range(4)], r=[bwkv, bckv], w=[bpvm])
        for tb in range(4):
            S.op("act", lambda e: e.activation(out=Vm[:, c * 4 + tb, 0:64], in_=pvm[:, tb, :], func=AF.Copy,
                                               scale=rkc[:, tb:tb + 1]), r=[bpvm, brkc], w=[bo])
        p, bp = fgroup(CKV + 64, 96, xi)
        p2, bp2 = fgroup(FILL, 96, xi)
        S.op("dve", lambda e: e.tensor_tensor(out=t1[64:96, :], in0=p[64:96, :], in1=cosT[xi][64:96, :], op=ALU.mult),
             r=[bp, bcs[xi]], w=[bt1])
        S.op("dve", lambda e: e.tensor_tensor(out=t2[64:96, :], in0=p2[64:96, :], in1=sinT[xi][64:96, :], op=ALU.mult),
             r=[bp2, bcs[xi]], w=[bt2])
        S.op("pool", lambda e: e.tensor_tensor(out=KTm[64:96, t0:t0 + 512], in0=t1[64:96, :], in1=t2[64:96, :], op=ALU.add),
             r=[bt1, bt2], w=[bo])
        for tb in range(4):
            k = tb // 2
            S.op("pe", [(lambda e, kc=kc: e.matmul(pv[k][:, tb % 2, 0:193], lhsT=xc[xi][:, kc, tb * 128:(tb + 1) * 128],
                                                   rhs=wsel[:, kc, FV:FV + 193], start=(kc == 0), stop=(kc == 7)))
                        for kc in range(8)], r=[bwsel, bxc[xi]], w=[bpv[k]])
            blk = c * 4 + tb
            S.op("act", lambda e: e.copy(out=Vf[:, blk, 0:64], in_=pv[k][:, tb % 2, 0:64]), r=[bpv[k]], w=[bo])
            S.op("dve", lambda e: e.tensor_copy(out=Fcol[:, blk:blk + 1], in_=pv[k][:, tb % 2, 64:65]), r=[bpv[k]], w=[bo])
            S.op("dve", lambda e: e.tensor_copy(out=Vd[:, blk, :], in_=pv[k][:, tb % 2, 65:193]), r=[bpv[k]], w=[bo])
    C.release(mark1)
    S.barrier()
    ball = Buf()
    L = C.sb("Lg", [128, NBLK], F32)
    bL = Buf()
    S.op("act", lambda e: e.activation(out=L[:], in_=Fcol[:], func=AF.Exp, bias=nfb[:], scale=-1.0), w=[bL])
    S.op("act", lambda e: e.activation(out=L[:], in_=L[:], func=AF.Ln, bias=1.0), r=[bL], w=[bL])
    pbc = C.ps("pbc", [128, 512], F32)
    bpbc = Buf()
    pcl = pbc[:, 0:2 * NBLK].rearrange("p (a n) -> p a n", a=2)
    bpcl = bpbc
    S.op("pe", lambda e: e.matmul(pcl[:, 0, :], lhsT=tri[:], rhs=L[:], start=True, stop=True), r=[bL], w=[bpcl])
    S.op("pe", lambda e: e.matmul(pcl[:, 1, :], lhsT=ones_f[:], rhs=L[:], start=True, stop=True), r=[bL, bpcl], w=[bpcl])
    tot = C.sb("tot", [128, NBLK], F32)
    incl = C.sb("incl", [128, NBLK], F32)
    cumL = C.sb("cumL", [128, NBLK], F32)
    bcum = Buf()
    S.op("dve", lambda e: e.tensor_copy(out=tot[:], in_=pcl[:, 1, :]), r=[bpcl], w=[bcum])
    S.op("dve", lambda e: e.tensor_tensor_scan(out=incl[:], data0=ones_f[:, 0:NBLK], data1=tot[:], initial=0.0,
                                               op0=ALU.mult, op1=ALU.add), r=[bcum], w=[bcum])
    S.op("dve", lambda e: e.tensor_tensor(out=cumL[:], in0=pcl[:, 0, :], in1=incl[:], op=ALU.add), r=[bpcl, bcum], w=[bcum])
    S.op("dve", lambda e: e.tensor_tensor(out=cumL[:], in0=cumL[:], in1=tot[:], op=ALU.subtract), r=[bcum], w=[bcum])
    tabF = C.sb("tabF", [128, NCH, NBLK], F32)
    tabFd = C.sb("tabFd", [128, NBLK], F32)
    for g in range(NCH):
        S.op("dve", lambda e: e.tensor_scalar(out=tabF[:, g, :], in0=cumL[:], scalar1=incl[:, 4 * g + 3:4 * g + 4], scalar2=None,
                                              op0=ALU.subtract), r=[bcum], w=[bcum])
    S.op("dve", lambda e: e.tensor_tensor(out=tabFd[:], in0=cumL[:], in1=incl[:], op=ALU.subtract), r=[bcum], w=[bcum])
    dvl = C.sb("dvl", [33, NBLK], F32)
    dhi = C.sb("dhi", [33, NBLK], BF16)
    dhf = C.sb("dhf", [33, NBLK], F32)
    dlo = C.sb("dlo", [33, NBLK], BF16)
    i4 = incl[0:33, :].rearrange("p (g m) -> p g m", m=4)
    d4 = dvl[:].rearrange("p (g m) -> p g m", m=4)
    for m in range(4):
        S.op("dve", lambda e: e.tensor_tensor(out=d4[:, :, m], in0=i4[:, :, 3], in1=i4[:, :, m], op=ALU.subtract), r=[bcum], w=[bcum])
    S.op("dve", lambda e: e.tensor_copy(out=dhi[:], in_=dvl[:]), r=[bcum], w=[bcum])
    S.op("dve", lambda e: e.tensor_copy(out=dhf[:], in_=dhi[:]), r=[bcum], w=[bcum])
    S.op("dve", lambda e: e.tensor_tensor(out=dhf[:], in0=dvl[:], in1=dhf[:], op=ALU.subtract), r=[bcum], w=[bcum])
    S.op("dve", lambda e: e.tensor_copy(out=dlo[:], in_=dhf[:]), r=[bcum], w=[bcum])
    S.barrier()
    ps_ = [C.ps(f"ps{i}", [128, 512], F32) for i in range(2)]
    bps = [Buf(), Buf()]
    pdg = C.ps("pdg", [128, 128], F32)
    bpdg = Buf()
    pacc = [C.ps(f"pacc{i}", [128, 512], F32) for i in range(2)]
    bpacc = [Buf(), Buf()]
    pden = [C.ps(f"pden{i}", [128, 512], F32) for i in range(2)]
    bpden = [Buf(), Buf()]
    pT = [C.sb(f"pT{i}", [128, 512], BF16) for i in range(2)]
    bpT = [Buf(), Buf()]
    pTd = C.sb("pTd", [128, 128], BF16)
    bpTd = Buf()
    tdg = C.sb("tdg", [128, 128], F32)
    btdg = Buf()
    qaF = [C.sb(f"qaF{i}", [64, 512], BF16) for i in range(2)]
    bqaF = [Buf(), Buf()]
    for i in range(2):
        S.op("pool", lambda e: e.memset(qaF[i][:], 0.0), w=[bqaF[i]])
    rec = C.sb("rec", [128, 512], F32)
    brec = Buf()
    osb = [C.sb(f"osb{i}", [128, 512], F32) for i in range(2)]
    bosb = [Buf(), Buf()]
    o1n = C.sb("o1n", [128, 512], F32)
    bo1n = Buf()
    od = C.sb("od", [128, 512], F32)
    bod = Buf()
    sqd = C.sb("sqd", [128, 512], BF16)
    bsqd = Buf()
    rsd = C.sb("rsd", [128, 512], F32)
    brsd = Buf()
    ost = [C.sb(f"ost{i}", [128, 512], BF16) for i in range(2)]
    bost = [Buf(), Buf()]
    nt = [0]
    nacc = [0]
    nst = [0]
    toks = []

    def attn_group(g, KT, QT, aug, bias_off, bias_diag, T, bT, Vst, Mv, den_sep):
        ai = nacc[0] % 2
        nacc[0] += 1
        acc, bacc = pacc[ai], bpacc[ai]
        dn, bdn = pden[ai], bpden[ai]
        nj = 4 * g + 4
        first = [True]
        pend = [None]
        q0 = g * 512

        def pv_mm(rhs_ap, c0, c1, rbufs, last, jj):
            fns = [lambda e: e.matmul(acc[0:Mv, c0:c1], lhsT=Vst(jj), rhs=rhs_ap, start=first[0], stop=last)]
            wl = [bacc]
            if den_sep:
                fns.append(lambda e: e.matmul(dn[0:1, c0:c1], lhsT=ones_bf[:, 0:1], rhs=rhs_ap, start=first[0], stop=last))
                wl.append(bdn)
            S.op("pe", fns, r=rbufs + [ball, bones], w=wl)
            first[0] = False

        for j in range(nj):
            m = j - 4 * g
            c0 = 0 if m < 0 else 128 * (m + 1)
            if c0 < 512:
                si = nt[0] % 2
                nt[0] += 1
                fns = [lambda e: e.matmul(ps_[si][:, c0:512], lhsT=KT(j), rhs=QT(q0 + c0, q0 + 512), start=True, stop=(aug is None))]
                rb = [ball]
                if aug is not None:
                    ka, qa, bq = aug(g, j, c0)
                    fns.append(lambda e: e.matmul(ps_[si][:, c0:512], lhsT=ka, rhs=qa, start=False, stop=True))
                    rb = rb + bq
                S.op("pe", fns, r=rb, w=[bps[si]])
                S.op("act", lambda e: e.activation(out=pT[si][:, c0:512], in_=ps_[si][:, c0:512], func=AF.Exp,
                                                   bias=bias_off(g, j)), r=[bps[si], ball], w=[bpT[si]])
                if pend[0] is not None:
                    pv_mm(*pend[0])
                    pend[0] = None
                if m < 0:
                    pend[0] = (pT[si][:, c0:512], c0, 512, [bpT[si]], False, j)
                else:
                    pv_mm(pT[si][:, c0:512], c0, 512, [bpT[si]], False, j)
            elif pend[0] is not None:
                pv_mm(*pend[0])
                pend[0] = None
            if m >= 0:
                S.op("pe", lambda e: e.matmul(pdg[:, :], lhsT=KT(j), rhs=QT(q0 + 128 * m, q0 + 128 * m + 128), start=True, stop=True),
                     r=[ball], w=[bpdg])
                S.op("dve", lambda e: e.tensor_tensor(out=tdg[:], in0=pdg[:, :], in1=T, op=ALU.add), r=[bpdg, bT], w=[btdg])
                bd = bias_diag(j)
                S.op("act", lambda e: e.activation(out=pTd[:], in_=tdg[:], func=AF.Exp, bias=bd), r=[btdg, ball], w=[bpTd])
                pv_mm(pTd[:], 128 * m, 128 * m + 128, [bpTd], j == nj - 1, j)
        assert pend[0] is None
        return ai

    def normalize(ai, Mv, den_row, out_ap, bout, extra_r=()):
        acc, bacc = pacc[ai], bpacc[ai]
        src, bsrc = (acc, bacc) if den_row > 0 else (pden[ai], bpden[ai])
        d0 = den_row
        S.op("dve", lambda e: e.reciprocal(out=rec[d0:d0 + 1, :], in_=src[d0:d0 + 1, :]), r=[bsrc], w=[brec])
        S.op("pe", lambda e: e.matmul(pbc[0:Mv, :], lhsT=ones_f[d0:d0 + 1, 0:Mv], rhs=rec[d0:d0 + 1, :], start=True, stop=True),
             r=[brec, bones], w=[bpbc])
        oi = nst[0] % 2
        S.op("act", lambda e: e.copy(out=osb[oi][0:Mv, :], in_=acc[0:Mv, :]), r=[bacc], w=[bosb[oi]])
        S.op("dve", lambda e: e.tensor_tensor(out=out_ap, in0=osb[oi][0:Mv, :], in1=pbc[0:Mv, :], op=ALU.mult),
             r=[bosb[oi], bpbc] + list(extra_r), w=[bout])

    for g in range(NCH):
        q0 = g * 512
        fi = g % 2
        for m in range(4):
            col = 4 * g + m
            S.op("dve", lambda e: e.tensor_copy(out=qaF[fi][0:32, m * 128:(m + 1) * 128],
                                                in_=dhi[0:32, col:col + 1].broadcast_to([32, 128])), r=[bcum], w=[bqaF[fi]])
            S.op("dve", lambda e: e.tensor_copy(out=qaF[fi][32:33, m * 128:(m + 1) * 128],
                                                in_=dlo[32:33, col:col + 1].broadcast_to([1, 128])), r=[bcum], w=[bqaF[fi]])
        ai = attn_group(g, lambda j: KTf[:, j * 128:(j + 1) * 128], lambda a, b: QTf[:, a:b],
                        lambda g_, j, c0: (kaugF[:, :], qaF[fi][:, c0:512], [bqaF[fi], bkaugF]),
                        lambda g_, j: tabF[:, g_, j:j + 1], lambda j: tabFd[:, j:j + 1], TF[:], bTF,
                        lambda j: Vf[:, j, :], 65, False)
        si = nst[0] % 2
        normalize(ai, 64, 64, ost[si][0:64, :], bost[si])
        toks.append(S.dma("pool", oT_d[0:64, q0:q0 + 512], ost[si][0:64, :], r=[bost[si]], slot=f"ost{si}"))
        nst[0] += 1
        ai = attn_group(g, lambda j: KTm[:, j * 128:(j + 1) * 128], lambda a, b: QTm[:, a:b], None,
                        lambda g_, j: 0.0, lambda j: 0.0, TM[:], bTM, lambda j: Vm[:, j, :], 65, False)
        si = nst[0] % 2
        normalize(ai, 64, 64, ost[si][0:64, :], bost[si])
        toks.append(S.dma("pool", oT_d[64:128, q0:q0 + 512], ost[si][0:64, :], r=[bost[si]], slot=f"ost{si}"))
        nst[0] += 1
        for mp in range(2):
            lo, hi = mp * 64, mp * 64 + 64
            ai = attn_group(g, lambda j: K12[lo:hi, j * 128:(j + 1) * 128], lambda a, b: Q12[lo:hi, a:b],
                            lambda g_, j, c0: (kaug[lo:hi, :], qaug[lo:hi, c0:512], [bqaug, bkaug]),
                            lambda g_, j: bsd[:, 4 * g_ - j + 3:4 * g_ - j + 4], lambda j: 0.0, TD[:], bTD,
                            lambda j: Vd[:, j, :], 128, True)
            if mp == 0:
                normalize(ai, 128, 0, o1n[:], bo1n)
            else:
                normalize(ai, 128, 0, od[:], bod)
            nst[0] += 1
        S.op("dve", lambda e: e.scalar_tensor_tensor(out=od[:], in0=od[:], scalar=nlam[:], in1=o1n[:], op0=ALU.mult, op1=ALU.add),
             r=[bod, bo1n, blam], w=[bod])
        S.op("act", lambda e: e.activation(out=sqd[:], in_=od[:], func=AF.Square), r=[bod], w=[bsqd])
        S.op("pe", lambda e: e.matmul(pbc[:, :], lhsT=ones_bf[:, :], rhs=sqd[:], start=True, stop=True), r=[bsqd, bones], w=[bpbc])
        S.op("act", lambda e: e.activation(out=rsd[:], in_=pbc[:, :], func=AF.Ln, bias=float(EPS), scale=1.0 / 128), r=[bpbc], w=[brsd])
        S.op("act", lambda e: e.activation(out=rsd[:], in_=rsd[:], func=AF.Exp, scale=-0.5), r=[brsd], w=[brsd])
        si = nst[0] % 2
        S.op("dve", lambda e: e.scalar_tensor_tensor(out=ost[si][:], in0=od[:], scalar=dgs[:], in1=rsd[:], op0=ALU.mult, op1=ALU.mult),
             r=[bod, brsd, bdgs], w=[bost[si]])
        toks.append(S.dma("pool", oT_d[128:256, q0:q0 + 512], ost[si][:], r=[bost[si]], slot=f"ost{si}"))
        nst[0] += 1
    S.finish(toks)
    S.barrier()
    C.release(mark0)
    return toks


def build_phaseA(SL, layer):
    nc = bass.Bass("TRN2", target_bir_lowering=False)
    dt = nc.dram_tensor
    I = "ExternalInput"
    xsT = dt("xsT", [8, 128, SL], BF16, kind=I).ap()
    w_sel = dt("w_sel", [D, NCOL], F32, kind=I).ap()
    ln1g = dt("ln1g", [128, 8], F32, kind=I).ap()
    wuq = dt("wuq", [256, 192], F32, kind=I).ap()
    qng = dt("qng", [128, 2], F32, kind=I).ap()
    wukv = dt("wukv", [128, 128], F32, kind=I).ap()
    kvng = dt("kvng", [128, 1], F32, kind=I).ap()
    dng = dt("dng", [128, 1], F32, kind=I).ap()
    fb = dt("fb", [128, 1], F32, kind=I).ap()
    lamv = dt("lamv", [256], F32, kind=I).ap()
    cosd = dt("cosd", [32, SL], F32, kind=I).ap()
    sind = dt("sind", [32, SL], F32, kind=I).ap()
    qaugd = dt("qaugd", [128, 512], BF16, kind=I).ap()
    kaugd = dt("kaugd", [128, 128], BF16, kind=I).ap()
    biasd = dt("biasd", [128, 64], F32, kind=I).ap()
    tdd = dt("tdd", [128, 128], F32, kind=I).ap()
    tfd = dt("tfd", [128, 128], F32, kind=I).ap()
    tmd = dt("tmd", [128, 128], F32, kind=I).ap()
    trid = dt("trid", [128, 128], F32, kind=I).ap()
    oT_d = dt("oT", [256, SL], BF16, kind="ExternalOutput").ap()
    C = Ctx(nc, "a")
    emit_phaseA(C, SL, layer, xsT, w_sel, ln1g, wuq, qng, wukv, kvng, dng, fb, lamv, cosd, sind, qaugd, kaugd, biasd,
                tdd, tfd, tmd, trid, oT_d)
    return nc


def host_consts(head, SL):
    bf = ml_dtypes.bfloat16
    slope = 2.0 ** (-8.0 * (head + 1) / 4)
    ql = np.arange(512)
    qa3 = np.stack([-slope * 128.0 * (ql // 128), -slope * (ql % 128), np.ones(512)]).astype(np.float32)
    kl = np.arange(128)
    ka3 = np.stack([np.ones(128), np.ones(128), slope * kl]).astype(np.float32)
    qaug = np.zeros((128, 512), np.float32)
    kaug = np.zeros((128, 128), np.float32)
    qaug[0:3] = qa3
    qaug[64:67] = qa3
    kaug[0:3] = ka3
    kaug[64:67] = ka3
    qaug = qaug.astype(bf)
    kaug = kaug.astype(bf)
    biasd = np.tile((-slope * 128.0 * (np.arange(64) - 3))[None, :], (128, 1)).astype(np.float32)
    k = np.arange(128)[:, None]
    q = np.arange(128)[None, :]
    vis = (k // 64) <= (q // 64)
    tdd = np.where(vis, -slope * np.abs(q - k), NEG).astype(np.float32)
    tfd = np.where(k <= q, 0.0, NEG).astype(np.float32)
    tmd = np.where(vis, 0.0, NEG).astype(np.float32)
    trid = (k <= q).astype(np.float32)
    inv = 1.0 / (10000.0 ** (np.arange(16, dtype=np.float32) / 16))
    ang = np.arange(SL, dtype=np.float32)[None, :] * inv[:, None]
    cos, sin = np.cos(ang).astype(np.float32), np.sin(ang).astype(np.float32)
    cosd = np.concatenate([cos, cos], 0)
    sind = np.concatenate([-sin, sin], 0)
    return dict(qaugd=qaug, kaugd=kaug, biasd=biasd, tdd=tdd, tfd=tfd, tmd=tmd, trid=trid,
                cosd=np.ascontiguousarray(cosd), sind=np.ascontiguousarray(sind))


def host_selA(inp, layer, head):
    w_in = inp["w_in"][layer]
    offs = np.cumsum([0, 256, 256, 256, 4, 512, 512, 512, 256, 128, 32])
    fq, fk, fv, ff, dq, dk, dv, cq, ckv, kr = [w_in[:, offs[i]:offs[i + 1]] for i in range(10)]
    h = head
    cols = [fq[:, h * 64:(h + 1) * 64], fk[:, h * 64:(h + 1) * 64], dq[:, h * 128:(h + 1) * 128], dk[:, h * 128:(h + 1) * 128],
            cq, ckv, kr, ckv[:, 0:64], kr[:, 16:32], kr[:, 0:16], fv[:, h * 64:(h + 1) * 64], ff[:, h:h + 1],
            dv[:, h * 128:(h + 1) * 128]]
    w_sel = np.ascontiguousarray(np.concatenate(cols, axis=1))
    assert w_sel.shape[1] == NCOL
    wuq_h = inp["w_uq"][layer][:, h * 96:(h + 1) * 96]
    wuq = np.ascontiguousarray(np.concatenate([wuq_h, wuq_h[:, 0:64], wuq_h[:, 80:96], wuq_h[:, 64:80]], axis=1))
    wukv = np.ascontiguousarray(inp["w_ukv"][layer][:, h * 128:(h + 1) * 128])
    lamv = np.concatenate([inp["lam_q1"][layer], inp["lam_k1"][layer], inp["lam_q2"][layer], inp["lam_k2"][layer]])
    return dict(w_sel=w_sel, ln1g=np.ascontiguousarray(inp["ln1_g"][layer].reshape(8, 128).T), wuq=wuq,
                qng=np.ascontiguousarray(inp["q_norm_g"][layer].reshape(2, 128).T), wukv=wukv,
                kvng=np.ascontiguousarray(inp["kv_norm_g"][layer].reshape(128, 1)),
                dng=np.ascontiguousarray(inp["diff_norm_g"][layer].reshape(128, 1)),
                fb=np.full((128, 1), inp["fgate_b"][layer][h], np.float32), lamv=np.ascontiguousarray(lamv))


def _wo_perm():
    idx = []
    for r in range(4):
        idx += list(range(r * 64, (r + 1) * 64))
        idx += list(range(768 + r * 64, 768 + (r + 1) * 64))
        idx += list(range(256 + r * 128, 256 + (r + 1) * 128))
    return np.array(idx)


def kernel(**inp):
    inp = {k: np.asarray(v) for k, v in inp.items()}
    bf = ml_dtypes.bfloat16
    x = np.ascontiguousarray(inp["x"], dtype=np.float32)
    cores = list(range(NCORE))
    consts = [host_consts(h, SEQ) for h in range(4)]
    xs = [np.ascontiguousarray(x[c // 4, (c % 4) * TOK:(c % 4 + 1) * TOK]) for c in cores]
    res = run_bass_kernel_spmd(build_phaseN(), [{"x": xs[c]} for c in cores], core_ids=cores).results
    xsT = [np.asarray(res[c]["xsT"]) for c in cores]
    perm = _wo_perm()
    out = None
    for layer in range(DEPTH):
        xsT_b = [np.ascontiguousarray(np.concatenate([xsT[b * 4 + j] for j in range(4)], axis=2)) for b in range(NB)]
        maps = []
        for c in cores:
            m = dict(xsT=xsT_b[c // 4])
            m.update(host_selA(inp, layer, c % 4))
            m.update(consts[c % 4])
            maps.append(m)
        res = run_bass_kernel_spmd(build_phaseA(SEQ, layer), maps, core_ids=cores).results
        oT_b = [np.concatenate([np.asarray(res[b * 4 + h]["oT"]) for h in range(4)], axis=0) for b in range(NB)]
        last = layer == DEPTH - 1
        w_o = np.ascontiguousarray(inp["w_o"][layer][perm])
        ln2 = np.ascontiguousarray(inp["ln2_g"][layer].reshape(8, 128).T)
        cw = np.ascontiguousarray(inp["conv_w"][layer].reshape(3, 44, 128).transpose(2, 0, 1))
        cb = np.ascontiguousarray(inp["conv_b"][layer].reshape(44, 128).T)
        maps = []
        for c in cores:
            b, j = c // 4, c % 4
            o = oT_b[b]
            oT = np.ascontiguousarray(o[:, j * TOK:(j + 1) * TOK].reshape(8, 128, TOK))
            if j == 0:
                oTh = np.zeros((8, 128, 128), bf)
                xh = np.zeros((128, D), np.float32)
            else:
                oTh = np.ascontiguousarray(o[:, j * TOK - 128:j * TOK].reshape(8, 128, 128))
                xh = np.ascontiguousarray(xs[c - 1][TOK - 128:])
            m = dict(x=xs[c], oT=oT, xh=xh, oTh=oTh, w_o=w_o, ln2_g=ln2, w_up=np.ascontiguousarray(inp["w_up"][layer]),
                     conv_w=cw, conv_b=cb, w_down=np.ascontiguousarray(inp["w_down"][layer]))
            if last:
                m["final_g"] = np.ascontiguousarray(inp["final_g"])
            maps.append(m)
        res = run_bass_kernel_spmd(build_phaseB(last), maps, core_ids=cores).results
        if last:
            out = np.stack([np.concatenate([np.asarray(res[b * 4 + j]["out"]) for j in range(4)], axis=0) for b in range(NB)])
        else:
            xs = [np.asarray(res[c]["xnew"]) for c in cores]
            xsT = [np.asarray(res[c]["xsT"]) for c in cores]
    return np.ascontiguousarray(out, dtype=np.float32)
```
